# Optimizing a Trainium2 kernel written in Bass

```python
import math
import jax, jax.numpy as jnp
from jax import lax
import numpy as np

D_MODEL = 1024
BATCH = 2
SEQ = 8192
DEPTH = 1

DA_HEADS = 4
DA_QK_DIM = 64
DA_V_DIM = 2 * DA_QK_DIM
DA_QK_WIDTH = DA_HEADS * 2 * DA_QK_DIM
DA_WIDTH = DA_HEADS * DA_V_DIM
DN_HEADS = 4
DN_K_DIM = 128
DN_V_DIM = 128
DN_WIDTH = DN_HEADS * DN_V_DIM
DN_QKV_WIDTH = DN_HEADS * (2 * DN_K_DIM + DN_V_DIM)
CONV_WIDTH = 4
CHUNK = 64
MIX_WIDTH = DA_WIDTH + DN_WIDTH
D_FF = 4 * D_MODEL
Q_BLOCK = 128
EPS = 1e-6
IN_WIDTH = 2 * DA_QK_WIDTH + DA_WIDTH + DN_QKV_WIDTH + DN_WIDTH + 2 * DN_HEADS
SPLIT_1 = DA_QK_WIDTH
SPLIT_2 = SPLIT_1 + DA_QK_WIDTH
SPLIT_3 = SPLIT_2 + DA_WIDTH
SPLIT_4 = SPLIT_3 + DN_QKV_WIDTH
SPLIT_5 = SPLIT_4 + DN_WIDTH
SPLIT_6 = SPLIT_5 + DN_HEADS

kernel_name = "hymba_diffattn_gdn_sqrelu"


def rms_norm(x, w):
    xf = x.astype(jnp.float32)
    y = xf * lax.rsqrt(jnp.mean(xf * xf, axis=-1, keepdims=True) + EPS)
    return (y * w.astype(jnp.float32)).astype(x.dtype)


def l2_norm(x):
    xf = x.astype(jnp.float32)
    return xf * lax.rsqrt(jnp.sum(xf * xf, axis=-1, keepdims=True) + EPS)


def diff_attention(q, k, v, lam):
    B, S = q.shape[0], q.shape[1]
    nb = S // Q_BLOCK
    scale = DA_QK_DIM ** -0.5
    k_pos = jnp.arange(S)
    q_blocks = jnp.moveaxis(q.reshape(B, nb, Q_BLOCK, DA_HEADS, 2, DA_QK_DIM), 1, 0)
    starts = jnp.arange(nb) * Q_BLOCK

    def block(args):
        qb, start = args
        s = jnp.einsum('bqhcd,bkhcd->bhcqk', qb, k,
                       preferred_element_type=jnp.float32) * scale
        causal = (start + jnp.arange(Q_BLOCK))[:, None] >= k_pos[None, :]
        s = jnp.where(causal, s, jnp.finfo(jnp.float32).min)
        p = jax.nn.softmax(s, axis=-1)
        a = p[:, :, 0] - lam * p[:, :, 1]
        return jnp.einsum('bhqk,bkhe->bqhe', a.astype(v.dtype), v)

    out = lax.map(block, (q_blocks, starts))
    return jnp.moveaxis(out, 0, 1).reshape(B, S, DA_HEADS, DA_V_DIM)


def causal_conv_silu(x, w):
    C = x.shape[-1]
    y = lax.conv_general_dilated(
        x, w[:, None, :].astype(x.dtype), window_strides=(1,),
        padding=[(CONV_WIDTH - 1, 0)], dimension_numbers=('NWC', 'WIO', 'NWC'),
        feature_group_count=C)
    return jax.nn.silu(y)


def gated_delta_rule(q, k, v, g, beta):
    B, S, H, Dk = k.shape
    Dv = v.shape[-1]
    n = S // CHUNK

    def chunks(t):
        t = jnp.moveaxis(t.astype(jnp.float32), 2, 1)
        return t.reshape((B, H, n, CHUNK) + t.shape[3:])

    q, k, v, g, beta = (chunks(q * Dk ** -0.5), chunks(k), chunks(v), chunks(g), chunks(beta))
    g = jnp.cumsum(g, axis=-1)
    tri = jnp.tril(jnp.ones((CHUNK, CHUNK), dtype=bool))
    strict = jnp.tril(jnp.ones((CHUNK, CHUNK), dtype=bool), -1)
    gdiff = g[..., :, None] - g[..., None, :]
    decay = jnp.where(tri, jnp.exp(jnp.where(tri, gdiff, 0.0)), 0.0)
    k_beta = k * beta[..., None]
    v_beta = v * beta[..., None]
    Lmat = jnp.where(strict, jnp.einsum('bhncd,bhnjd->bhncj', k_beta, k) * decay, 0.0)
    rhs = jnp.concatenate([v_beta, k_beta * jnp.exp(g)[..., None]], axis=-1)
    sol = lax.linalg.triangular_solve(Lmat, rhs, left_side=True, lower=True, unit_diagonal=True)
    u, w = sol[..., :Dv], sol[..., Dv:]
    qk = jnp.where(tri, jnp.einsum('bhncd,bhnjd->bhncj', q, k) * decay, 0.0)
    xs = tuple(jnp.moveaxis(t, 2, 0) for t in (q, k, u, w, qk, g))

    def step(state, inp):
        q_c, k_c, u_c, w_c, qk_c, g_c = inp
        v_new = u_c - jnp.einsum('bhcd,bhde->bhce', w_c, state)
        o = (jnp.einsum('bhcd,bhde->bhce', q_c * jnp.exp(g_c)[..., None], state)
             + jnp.einsum('bhcj,bhje->bhce', qk_c, v_new))
        g_last = g_c[..., -1]
        k_dec = k_c * jnp.exp(g_last[..., None] - g_c)[..., None]
        state = state * jnp.exp(g_last)[..., None, None] + jnp.einsum('bhcd,bhce->bhde', k_dec, v_new)
        return state, o

    state0 = jnp.zeros((B, H, Dk, Dv), jnp.float32)
    _, o = lax.scan(step, state0, xs)
    o = jnp.moveaxis(o, 0, 2).reshape(B, H, S, Dv)
    return jnp.moveaxis(o, 1, 2)


def setup_inputs(seed: int = 0) -> dict:
    key = jax.random.key(seed)
    ks = jax.random.split(key, 20)
    f32 = jnp.float32
    nrm = lambda k, shape, s: jax.random.normal(k, shape, f32) * s
    dt = jnp.exp(jax.random.uniform(ks[12], (DEPTH, DN_HEADS), f32, math.log(1e-3), math.log(1e-1)))
    return {
        "x": jax.random.normal(ks[0], (BATCH, SEQ, D_MODEL), f32),
        "norm1_w": 1.0 + nrm(ks[1], (DEPTH, D_MODEL), 0.02),
        "w_in": nrm(ks[2], (DEPTH, D_MODEL, IN_WIDTH), D_MODEL ** -0.5),
        "lambda_q1": nrm(ks[3], (DEPTH, DA_QK_DIM), 0.1),
        "lambda_k1": nrm(ks[4], (DEPTH, DA_QK_DIM), 0.1),
        "lambda_q2": nrm(ks[5], (DEPTH, DA_QK_DIM), 0.1),
        "lambda_k2": nrm(ks[6], (DEPTH, DA_QK_DIM), 0.1),
        "q_norm_w": 1.0 + nrm(ks[7], (DEPTH, DA_QK_DIM), 0.02),
        "k_norm_w": 1.0 + nrm(ks[8], (DEPTH, DA_QK_DIM), 0.02),
        "da_out_norm_w": 1.0 + nrm(ks[9], (DEPTH, DA_V_DIM), 0.02),
        "conv_w": nrm(ks[10], (DEPTH, CONV_WIDTH, DN_QKV_WIDTH), CONV_WIDTH ** -0.5),
        "A_log": jnp.log(jax.random.uniform(ks[11], (DEPTH, DN_HEADS), f32, 1.0, 16.0)),
        "dt_bias": dt + jnp.log(-jnp.expm1(-dt)),
        "dn_out_norm_w": 1.0 + nrm(ks[13], (DEPTH, DN_V_DIM), 0.02),
        "w_out": nrm(ks[14], (DEPTH, MIX_WIDTH, D_MODEL), MIX_WIDTH ** -0.5),
        "norm2_w": 1.0 + nrm(ks[15], (DEPTH, D_MODEL), 0.02),
        "w_up": nrm(ks[16], (DEPTH, D_MODEL, D_FF), D_MODEL ** -0.5),
        "w_down": nrm(ks[17], (DEPTH, D_FF, D_MODEL), D_FF ** -0.5),
    }


def reference(x, norm1_w, w_in, lambda_q1, lambda_k1, lambda_q2, lambda_k2, q_norm_w, k_norm_w,
              da_out_norm_w, conv_w, A_log, dt_bias, dn_out_norm_w, w_out, norm2_w, w_up, w_down):
    B, S = x.shape[0], x.shape[1]
    for l in range(DEPTH):
        h = rms_norm(x, norm1_w[l])
        proj = h @ w_in[l]
        da_q, da_k, da_v, dn_qkv, dn_z, dn_a, dn_b = jnp.split(
            proj, [SPLIT_1, SPLIT_2, SPLIT_3, SPLIT_4, SPLIT_5, SPLIT_6], axis=-1)

        da_q = rms_norm(da_q.reshape(B, S, DA_HEADS, 2, DA_QK_DIM), q_norm_w[l])
        da_k = rms_norm(da_k.reshape(B, S, DA_HEADS, 2, DA_QK_DIM), k_norm_w[l])
        da_v = da_v.reshape(B, S, DA_HEADS, DA_V_DIM)
        lam_init = 0.8 - 0.6 * math.exp(-0.3 * l)
        lam = (jnp.exp(jnp.sum(lambda_q1[l].astype(jnp.float32) * lambda_k1[l].astype(jnp.float32)))
               - jnp.exp(jnp.sum(lambda_q2[l].astype(jnp.float32) * lambda_k2[l].astype(jnp.float32)))
               + lam_init)
        da_o = diff_attention(da_q, da_k, da_v, lam)
        da_o = rms_norm(da_o, da_out_norm_w[l]) * (1.0 - lam_init)

        dn_qkv = causal_conv_silu(dn_qkv, conv_w[l])
        dn_q, dn_k, dn_v = jnp.split(dn_qkv, [DN_HEADS * DN_K_DIM, 2 * DN_HEADS * DN_K_DIM], axis=-1)
        dn_q = l2_norm(dn_q.reshape(B, S, DN_HEADS, DN_K_DIM))
        dn_k = l2_norm(dn_k.reshape(B, S, DN_HEADS, DN_K_DIM))
        dn_v = dn_v.reshape(B, S, DN_HEADS, DN_V_DIM)
        beta = jax.nn.sigmoid(dn_b.astype(jnp.float32))
        g = -jnp.exp(A_log[l].astype(jnp.float32)) * jax.nn.softplus(
            dn_a.astype(jnp.float32) + dt_bias[l].astype(jnp.float32))
        dn_o = gated_delta_rule(dn_q, dn_k, dn_v, g, beta).astype(x.dtype)
        dn_o = rms_norm(dn_o, dn_out_norm_w[l]) * jax.nn.silu(dn_z.reshape(B, S, DN_HEADS, DN_V_DIM))

        mix = jnp.concatenate([da_o.reshape(B, S, DA_WIDTH), dn_o.reshape(B, S, DN_WIDTH)], axis=-1)
        x = x + mix @ w_out[l]

        h = rms_norm(x, norm2_w[l])
        x = x + jnp.square(jax.nn.relu(h @ w_up[l])) @ w_down[l]
    return x
```

```python
import numpy as np
import ml_dtypes
import concourse.bass as bass
import concourse.mybir as mybir
from concourse.bass_utils import run_bass_kernel_spmd

F32 = mybir.dt.float32
BF16 = mybir.dt.bfloat16
I32 = mybir.dt.int32
AF = mybir.ActivationFunctionType
ALU = mybir.AluOpType
AX = mybir.AxisListType

D = 1024
EPS = 1e-6
LAM_INIT = 0.2
NEGBIG = -30000.0
ENG = ("pe", "act", "dve", "pool", "sp")


class Buf:
    __slots__ = ("name", "w", "r", "excl")

    def __init__(self, name, excl=False):
        self.name = name
        self.w = None
        self.r = {}
        self.excl = excl


class Op:
    __slots__ = ("eng", "fn", "deps", "signal", "sigval", "dma", "chanval", "idx", "inc")


class Sched:
    def __init__(self):
        self.q = {e: [] for e in ENG}
        self.chans = {}
        self.n = 0

    def barrier(self):
        lasts = []
        for e in ENG:
            comp = [o for o in self.q[e] if o.dma is None and o.fn is not None]
            if comp:
                comp[-1].signal = True
                lasts.append(comp[-1])
        seen = {}
        for e in ENG:
            for o in self.q[e]:
                if o.dma is not None:
                    seen[o.dma] = o
        lasts += list(seen.values())
        for e in ENG:
            o = Op()
            o.eng = e
            o.fn = None
            o.signal = False
            o.sigval = 0
            o.dma = None
            o.chanval = 0
            o.inc = 0
            o.idx = self.n
            self.n += 1
            o.deps = [d for d in lasts if not (d.dma is None and d.eng == e)]
            self.q[e].append(o)

    def op(self, eng, fn, r=(), w=(), dma=None, inc=16):
        o = Op()
        o.inc = inc
        o.eng = eng
        o.fn = fn
        o.signal = False
        o.sigval = 0
        o.dma = dma
        o.chanval = 0
        o.idx = self.n
        self.n += 1
        if dma is not None:
            c = self.chans.setdefault(dma, [0])
            c[0] += 1
            c[0] += 0
            o.chanval = inc * c[0]
        deps = {}

        def add(d, raw):
            if d is None or d is o:
                return
            if d.dma is None and d.eng == eng and dma is None:
                if eng == "pe" or not raw:
                    return
            key = ("d", d.dma) if d.dma is not None else ("e", d.eng)
            cur = deps.get(key)
            if cur is None or cur.idx < d.idx:
                deps[key] = d

        for b in r:
            add(b.w, True)
            if b.excl:
                for x in b.r.values():
                    add(x, False)
        for b in w:
            add(b.w, False)
            for x in b.r.values():
                add(x, False)
        o.deps = list(deps.values())
        for d in o.deps:
            d.signal = True
        for b in r:
            key = ("d", dma, o.idx) if dma is not None else eng
            b.r[key] = o
        for b in w:
            b.w = o
            b.r = {}
        self.q[eng].append(o)
        return o

    def emit(self, nc, block, sems, chan_sems):
        for e in ENG:
            n = 0
            for o in self.q[e]:
                if o.dma is None and o.signal:
                    n += 1
                    o.sigval = n

        def run(engname):
            def body(eobj):
                waited = {}
                for o in self.q[engname]:
                    for d in o.deps:
                        if d.dma is not None:
                            key = ("d", d.dma)
                            sem = chan_sems[d.dma]
                            val = d.chanval
                        else:
                            key = ("e", d.eng)
                            sem = sems[d.eng]
                            val = d.sigval
                        if waited.get(key, 0) >= val:
                            continue
                        eobj.wait_ge(sem, val)
                        waited[key] = val
                    if o.fn is None:
                        continue
                    ins = o.fn(eobj)
                    if o.dma is not None:
                        ins.then_inc(chan_sems[o.dma], o.inc)
                    elif o.signal:
                        ins.then_inc(sems[engname], 1)
            return body

        block.tensor(run("pe"))
        block.scalar(run("act"))
        block.vector(run("dve"))
        block.gpsimd(run("pool"))
        block.sync(run("sp"))


class Arena:
    def __init__(self, nc, base=16640, limit=229376):
        self.nc = nc
        self.off = base
        self.limit = limit
        self.k = 0

    def alloc(self, name, shape, dtype):
        size = 2 if dtype == BF16 else 4
        n = 1
        for s in shape[1:]:
            n *= s
        nbytes = n * size
        self.off = (self.off + 63) // 64 * 64
        assert self.off + nbytes <= self.limit, f"SBUF overflow at {name}: {self.off + nbytes}"
        self.k += 1
        t = self.nc.alloc_sbuf_tensor_at(f"{name}_{self.k}", list(shape), dtype, offset=self.off)
        self.off += nbytes
        return t


def build_program(S=8192, debug=False, stage=99):
    nc = bass.Bass("TRN2", target_bir_lowering=False)
    NBLK = S // 512
    NT = S // 128
    TC = S // 4
    CB = 256
    NCB = TC // CB

    x_d = nc.dram_tensor("x", [S, D], F32, kind="ExternalInput").ap()
    xres_d = nc.dram_tensor("xres", [TC, D], F32, kind="ExternalInput").ap()
    win_d = nc.dram_tensor("w_in", [D, 898], F32, kind="ExternalInput").ap()
    cst_d = nc.dram_tensor("consts", [128, 896], F32, kind="ExternalInput").ap()
    pp_d = nc.dram_tensor("pp", [128, 544], F32, kind="ExternalInput").ap()
    wout_d = nc.dram_tensor("w_out", [D, D], F32, kind="ExternalInput").ap()
    wup_d = nc.dram_tensor("w_up", [D, 4 * D], F32, kind="ExternalInput").ap()
    wdn_d = nc.dram_tensor("w_down", [4 * D, D], F32, kind="ExternalInput").ap()
    off_d = nc.dram_tensor("tokoff", [1, 1], I32, kind="ExternalInput").ap()
    y_d = nc.dram_tensor("y", [TC, D], F32, kind="ExternalOutput").ap()
    NCOL = max(1, TC // 512)
    PS = TC // NCOL
    mix_src = [nc.dram_tensor(f"mix_src{i}", [4 * PS, 256], BF16) for i in range(NCOL)]
    mix_all = [nc.dram_tensor(f"mix_all{i}", [16 * PS, 256], BF16) for i in range(NCOL)]
    mix_mine = nc.dram_tensor("mix_mine", [4 * (S // 4), 256], BF16)
    if debug:
        dbg_d = nc.dram_tensor("dbg", [S, 256], BF16, kind="ExternalOutput").ap()

    sc = Sched()
    A = Arena(nc)


    def finish():
        sc.barrier()
        from contextlib import ExitStack
        with ExitStack() as es:
            sems = {e: es.enter_context(nc.semaphore(f"sem_{e}")) for e in ENG}
            chan_sems = {c: es.enter_context(nc.semaphore(f"ch_{c}")) for c in sc.chans}
            block = es.enter_context(nc.Block())
            sc.emit(nc, block, sems, chan_sems)
        info = dict(n_ops=sc.n, per_eng={e: len(sc.q[e]) for e in ENG}, off=A.off)
        return nc, info

    def mk(eng):
        def f(method, r, w, **kw):
            return sc.op(eng, lambda e, m=method, kw=kw: getattr(e, m)(**kw), r, w)
        return f

    PE, ACT, DVE, POOL = mk("pe"), mk("act"), mk("dve"), mk("pool")

    def DMA(q, chan, out, in_, r, w):
        return sc.op(q, lambda e, o=out, i=in_: e.dma_start(out=o, in_=i), r, w, dma=chan)

    ps = [nc.alloc_psum_tensor(f"ps{i}", [128, 512], F32) for i in range(8)]
    psB = [Buf(f"ps{i}", excl=True) for i in range(8)]
    gctr = [0]
    glo = [4]

    def gbank():
        n = 8 - glo[0]
        i = glo[0] + (gctr[0] % n)
        gctr[0] += 1
        return i

    cst = A.alloc("cst", [128, 896], F32)
    cstb = A.alloc("cstb", [128, 896], BF16)
    pp = A.alloc("pp", [128, 544], F32)
    dv = A.alloc("dv", [128, 320], F32)
    epsT = A.alloc("epsT", [128, 8], F32)
    tmp64 = A.alloc("tmp64", [128, 64], F32)
    Bc, Bcb, Bpp, Bdv, Beps, Btmp64 = (Buf(n) for n in ("cst", "cstb", "pp", "dv", "eps", "tmp64"))

    DMA("sp", "cst", cst[:, :], cst_d[:, :], [], [Bc])
    DMA("sp", "pp", pp[:, :], pp_d[:, :], [], [Bpp])
    DVE("tensor_copy", [Bc], [Bcb], out=cstb[:, :], in_=cst[:, :])
    ident_f, triU_f, ones_f, negones_f, NEG_f, strictU_f = (cst[:, i * 128:(i + 1) * 128] for i in range(6))
    ident_b = cstb[:, 0:128]
    triU_b = cstb[:, 128:256]
    ones_b = cstb[:, 256:384]
    blk_b = cstb[:, 768:896]

    epsvals = [1024 * EPS, 64 * EPS, EPS, 128 * EPS, 1.0]
    for i, v in enumerate(epsvals):
        DVE("memset", [], [Beps], ap=epsT[:, i:i + 1], constant=v)
    EPS_X, EPS_QK, EPS_L2, EPS_O, ONE = (epsT[:, i:i + 1] for i in range(5))

    def tsmul(eng, r, w, out, in0, s):
        return eng("tensor_scalar", r, w, out=out, in0=in0, scalar1=s, scalar2=None, op0=ALU.mult)

    tsmul(DVE, [Bpp], [Bdv], dv[:, 0:16], pp[:, 0:16], 32.0)
    tsmul(DVE, [Bpp], [Bdv], dv[:, 16:17], pp[:, 17:18], 8.0)
    ACT("activation", [Bpp], [Bdv], out=dv[:, 25:26], in_=pp[:, 30:31], func=AF.Exp)
    tsmul(DVE, [Bdv], [Bdv], dv[:, 17:18], dv[:, 25:26], -1.0)
    DVE("tensor_tensor", [Bpp], [Btmp64], out=tmp64[:, :], in0=pp[:, 32:96], in1=pp[:, 96:160], op=ALU.mult)
    DVE("reduce_sum", [Btmp64], [Bdv], out=dv[:, 20:21], in_=tmp64[:, :], axis=AX.X)
    DVE("tensor_tensor", [Bpp, Bdv], [Btmp64], out=tmp64[:, :], in0=pp[:, 160:224], in1=pp[:, 224:288], op=ALU.mult)
    DVE("reduce_sum", [Btmp64], [Bdv], out=dv[:, 21:22], in_=tmp64[:, :], axis=AX.X)
    ACT("activation", [Bdv], [Bdv], out=dv[:, 22:24], in_=dv[:, 20:22], func=AF.Exp)
    DVE("tensor_tensor", [Bdv], [Bdv], out=dv[:, 24:25], in0=dv[:, 23:24], in1=dv[:, 22:23], op=ALU.subtract)
    DVE("tensor_scalar", [Bdv], [Bdv], out=dv[:, 18:19], in0=dv[:, 24:25], scalar1=-LAM_INIT, scalar2=None, op0=ALU.add)
    tsmul(DVE, [Bpp], [Bdv], dv[:, 64:192], pp[:, 288:416], float(np.sqrt(128.0) * (1.0 - LAM_INIT)))
    tsmul(DVE, [Bpp], [Bdv], dv[:, 192:320], pp[:, 416:544], float(np.sqrt(128.0)))
    n1s = dv[:, 0:8]
    n2s = dv[:, 8:16]
    wq = pp[:, 16:17]
    wk8 = dv[:, 16:17]
    negA = dv[:, 17:18]
    neglam = dv[:, 18:19]
    dtb = pp[:, 31:32]
    da_wp = dv[:, 64:192]
    dn_wp = dv[:, 192:320]

    mark_common = A.off
    if stage == 0:
        return finish()

    def rsqrt_act(out_ap, in_ap, eps_ap, r, w):
        ACT("activation", r + [Beps], w, out=out_ap, in_=in_ap, func=AF.Ln, bias=eps_ap, scale=1.0)
        ACT("activation", w, w, out=out_ap, in_=out_ap, func=AF.Exp, scale=-0.5)

    def sigmoid_act(out_ap, in_ap, r, w):
        ACT("activation", r, w, out=out_ap, in_=in_ap, func=AF.Exp, scale=-1.0)
        ACT("activation", w + [Beps], w, out=out_ap, in_=out_ap, func=AF.Ln, bias=ONE, scale=1.0)
        ACT("activation", w, w, out=out_ap, in_=out_ap, func=AF.Exp, scale=-1.0)

    Wb = A.alloc("Wb", [128, 8, 898], BF16)
    wst = [A.alloc(f"wst{i}", [128, 898], F32) for i in range(2)]
    Bwst = [Buf(f"wst{i}") for i in range(2)]
    BWb = Buf("Wb")
    for kc in range(8):
        s = kc % 2
        DMA("sp", f"wst{s}", wst[s][:, :], win_d[kc * 128:(kc + 1) * 128, :], [], [Bwst[s]])
        tsmul(DVE, [Bwst[s], Bdv], [BWb], Wb[:, kc, :], wst[s][:, :], n1s[:, kc:kc + 1])

    xt = [A.alloc(f"xt{i}", [128, 4, D], F32) for i in range(2)]
    Bxt = [Buf(f"xt{i}") for i in range(2)]
    junk = A.alloc("junk", [128, D], BF16)
    Bjunk = Buf("junk")
    xb = A.alloc("xb", [128, 4, D], BF16)
    Bxb = Buf("xb")
    hT = A.alloc("hT", [128, 8, 512], BF16)
    BhT = Buf("hT")
    stx = A.alloc("stx", [128, 16], F32)
    Bstx = Buf("stx")
    KT = A.alloc("KT", [128, S], BF16)
    BKT = [Buf(f"KT{i}") for i in range(NBLK)]
    Vaug = A.alloc("Vaug", [128, NT, 130], BF16)
    BV = [Buf(f"V{i}") for i in range(NBLK)]
    QTp = [A.alloc(f"QTp{m}", [128, 512], BF16) for m in range(2)]
    BQT = Buf("QTp")
    sqb = A.alloc("sqb", [128, 512], BF16)
    Bsqb = Buf("sqb")
    rr = A.alloc("rr", [128, 512], F32)
    Brr = Buf("rr")
    cbuf = [A.alloc(f"cbuf{i}", [128, 515], F32) for i in range(3)]
    Bcbf = [Buf(f"cbuf{i}") for i in range(3)]
    cy = [A.alloc(f"cy{i}", [128, 512], F32) for i in range(3)]
    Bcy = [Buf(f"cy{i}") for i in range(3)]
    sg = [A.alloc(f"sg{i}", [128, 512], F32) for i in range(3)]
    Bsg = [Buf(f"sg{i}") for i in range(3)]
    kq = A.alloc("kq", [128, 4, 256], BF16)
    Bkq = Buf("kq")
    vTd = A.alloc("vTd", [128, 512], BF16)
    BvTd = Buf("vTd")
    gw = A.alloc("gw", [128, 4, 128], F32)
    Bgw = Buf("gw")
    zt = A.alloc("zt", [128, 4, 128], F32)
    Bzt = Buf("zt")
    tk = A.alloc("tk", [128, 64], F32)
    Btk = Buf("tk")
    PT = [A.alloc(f"PT{i}", [128, 512], BF16) for i in range(2)]
    BPT = [Buf(f"PT{i}") for i in range(2)]
    mixo = [A.alloc(f"mixo{i}", [128, 4, 256], BF16) for i in range(2)]
    Bmixo = [Buf(f"mixo{i}") for i in range(2)]
    ot = A.alloc("ot", [128, 128], F32)
    oa = A.alloc("oa", [128, 128], F32)
    Bot, Boa = Buf("ot"), Buf("oa")
    stA = A.alloc("stA", [128, 8], F32)
    BstA = Buf("stA")
    Sst = A.alloc("Sst", [128, 128], F32)
    Sbf = A.alloc("Sbf", [128, 128], BF16)
    BSst, BSbf = Buf("Sst"), Buf("Sbf")
    BMIX = [Buf(f"mix_src{i}") for i in range(NCOL)]
    G = []
    for p in range(2):
        g = {}
        for nm, dt_, w_ in (("ke", BF16, 128), ("kdec", BF16, 128), ("vt", BF16, 128), ("gTri", F32, 128),
                            ("decT", F32, 128), ("decS", F32, 128), ("Nn", BF16, 128), ("NT", BF16, 128),
                            ("P0", BF16, 256), ("P1", BF16, 256), ("X0", BF16, 128), ("X1", BF16, 128),
                            ("qkm", BF16, 128), ("ub", F32, 128), ("wT", BF16, 128), ("vnew", BF16, 128),
                            ("Bs", F32, 128), ("og", F32, 128), ("stG", F32, 8)):
            g[nm] = A.alloc(f"{nm}{p}", [128, w_], dt_)
            g["B" + nm] = Buf(f"{nm}{p}")
        G.append(g)
    phaseA_end = A.off

    for i in range(3):
        POOL("memset", [], [Bcbf[i]], ap=cbuf[i][:, 0:3], constant=0.0)
    POOL("memset", [], BV, ap=Vaug[:, :, 128:129], constant=1.0)
    DVE("memset", [], [BSst], ap=Sst[:, :], constant=0.0)
    DVE("memset", [], [BSbf], ap=Sbf[:, :], constant=0.0)
    for m in range(2):
        DVE("memset", [], [BQT], ap=QTp[m][:, :], constant=0.0)

    def load_x(blk):
        s = blk % 2
        src = x_d[blk * 512:(blk + 1) * 512, :].rearrange("(t p) d -> p t d", p=128)
        for t in range(4):
            DMA("sp", f"xt{s}", xt[s][:, t, :], src[:, t, :], [], [Bxt[s]])

    load_x(0)
    stcnt = [0]

    for blk in range(NBLK):
        s = blk % 2
        ms = blk % 2
        if blk + 1 < NBLK:
            load_x(blk + 1)
        for t in range(4):
            ACT("activation", [Bxt[s]], [Bjunk, Bstx], out=junk[:, :], in_=xt[s][:, t, :], func=AF.Square, accum_out=stx[:, t:t + 1])
        rsqrt_act(stx[:, 4:8], stx[:, 0:4], EPS_X, [Bstx], [Bstx])
        for t in range(4):
            tsmul(DVE if t % 2 == 0 else POOL, [Bxt[s], Bstx], [Bxb], xb[:, t, :], xt[s][:, t, :], stx[:, 4 + t:5 + t])
        for kc in range(8):
            b = gbank()
            pv = ps[b][:, :].bitcast(BF16)
            for t in range(4):
                PE("transpose", [Bxb, Bcb], [psB[b]], out=pv[:, t * 128:(t + 1) * 128], in_=xb[:, t, kc * 128:(kc + 1) * 128], identity=ident_b)
            if kc % 2 == 0:
                ACT("activation", [psB[b]], [BhT], out=hT[:, kc, :], in_=pv[:, 0:512], func=AF.Copy)
            else:
                DVE("tensor_copy", [psB[b]], [BhT], out=hT[:, kc, :], in_=pv[:, 0:512])

        if stage == 1:
            return finish()
        def fm_proj(g, b):
            for kc in range(8):
                PE("matmul", [BWb, BhT], [psB[b]], out=ps[b][:, :], lhsT=Wb[:, kc, g * 128:(g + 1) * 128], rhs=hT[:, kc, :], start=(kc == 0), stop=(kc == 7))

        for g in (0, 1):
            b = gbank()
            fm_proj(g, b)
            ACT("activation", [psB[b]], [Bsqb], out=sqb[:, :], in_=ps[b][:, :], func=AF.Square)
            b2 = gbank()
            PE("matmul", [Bsqb, Bcb], [psB[b2]], out=ps[b2][:, :], lhsT=blk_b, rhs=sqb[:, :], start=True, stop=True)
            rsqrt_act(rr[:, :], ps[b2][:, :], EPS_QK, [psB[b2]], [Brr])
            if g == 0:
                for m in range(2):
                    lo, hi = m * 64, (m + 1) * 64
                    DVE("scalar_tensor_tensor", [psB[b], Brr, Bpp], [BQT], out=QTp[m][lo:hi, :], in0=ps[b][lo:hi, :], scalar=pp[lo:hi, 16:17], in1=rr[lo:hi, :],
                        op0=ALU.mult, op1=ALU.mult)
            else:
                DVE("scalar_tensor_tensor", [psB[b], Brr, Bdv], [BKT[blk]], out=KT[:, blk * 512:(blk + 1) * 512], in0=ps[b][:, :], scalar=wk8, in1=rr[:, :],
                    op0=ALU.mult, op1=ALU.mult)

        for ci, g in enumerate((2, 3, 4)):
            b = gbank()
            fm_proj(g, b)
            cb_, Bc_ = cbuf[ci], Bcbf[ci]
            ACT("activation", [psB[b]], [Bc_], out=cb_[:, 3:515], in_=ps[b][:, :], func=AF.Copy)
            tap = 18 + 4 * ci
            tsmul(POOL, [Bc_, Bpp], [Bcy[ci]], cy[ci][:, :], cb_[:, 0:512], pp[:, tap:tap + 1])
            for j in range(1, 4):
                DVE("scalar_tensor_tensor", [Bc_, Bpp, Bcy[ci]], [Bcy[ci]], out=cy[ci][:, :], in0=cb_[:, j:j + 512], scalar=pp[:, tap + j:tap + j + 1],
                     in1=cy[ci][:, :], op0=ALU.mult, op1=ALU.add)
            POOL("tensor_copy", [Bc_, Bcy[ci]], [Bc_], out=cb_[:, 0:3], in_=cb_[:, 512:515])
            sigmoid_act(sg[ci][:, :], cy[ci][:, :], [Bcy[ci]], [Bsg[ci]])
            if ci == 2:
                DVE("tensor_tensor", [Bcy[ci], Bsg[ci]], [BvTd], out=vTd[:, :], in0=cy[ci][:, :], in1=sg[ci][:, :], op=ALU.mult)
            else:
                DVE("tensor_tensor", [Bcy[ci], Bsg[ci]], [Bcy[ci]], out=cy[ci][:, :], in0=cy[ci][:, :], in1=sg[ci][:, :], op=ALU.mult)
                ACT("activation", [Bcy[ci]], [Bsqb], out=sqb[:, :], in_=cy[ci][:, :], func=AF.Square)
                b2 = gbank()
                PE("matmul", [Bsqb, Bcb], [psB[b2]], out=ps[b2][:, :], lhsT=ones_b, rhs=sqb[:, :], start=True, stop=True)
                rsqrt_act(sg[ci][:, :], ps[b2][:, :], EPS_L2, [psB[b2]], [Bsg[ci]])
                col = 128 if ci == 0 else 0
                scl = float(128.0 ** -0.5) if ci == 0 else 1.0
                DVE("scalar_tensor_tensor", [Bcy[ci], Bsg[ci]], [Bkq], out=kq[:, :, col:col + 128], in0=cy[ci][:, :].rearrange("p (c t) -> p c t", c=4),
                    scalar=scl, in1=sg[ci][:, :].rearrange("p (c t) -> p c t", c=4), op0=ALU.mult, op1=ALU.mult)

        bab = gbank()
        for t in range(4):
            for kc in range(8):
                PE("matmul", [BWb, BhT], [psB[bab]], out=ps[bab][:, t * 2:t * 2 + 2], lhsT=hT[:, kc, t * 128:(t + 1) * 128], rhs=Wb[:, kc, 896:898],
                   start=(kc == 0), stop=(kc == 7), skip_group_check=True)
        ab3 = ps[bab][:, 0:8].rearrange("p (t c) -> p t c", c=2)
        ACT("activation", [psB[bab], Bpp], [Btk], out=tk[:, 32:36], in_=ab3[:, :, 0], func=AF.Exp, bias=dtb, scale=1.0)
        ACT("activation", [Btk, Beps], [Btk], out=tk[:, 32:36], in_=tk[:, 32:36], func=AF.Ln, bias=ONE, scale=1.0)
        tsmul(DVE, [Btk, Bdv], [Btk], tk[:, 0:4], tk[:, 32:36], negA)
        sigmoid_act(tk[:, 4:8], ab3[:, :, 1], [psB[bab]], [Btk])
        tsmul(DVE, [Btk], [Btk], tk[:, 28:32], tk[:, 4:8], -1.0)
        for half in range(2):
            b = gbank()
            for tt in range(2):
                t = half * 2 + tt
                for kc in range(8):
                    PE("matmul", [BWb, BhT], [psB[b]], out=ps[b][:, tt * 256:(tt + 1) * 256], lhsT=hT[:, kc, t * 128:(t + 1) * 128], rhs=Wb[:, kc, 640:896],
                       start=(kc == 0), stop=(kc == 7), skip_group_check=True)
            for tt in range(2):
                t = half * 2 + tt
                tile_i = blk * 4 + t
                ACT("activation", [psB[b]], [BV[blk]], out=Vaug[:, tile_i, 0:128], in_=ps[b][:, tt * 256:tt * 256 + 128], func=AF.Copy)
                zp = ps[b][:, tt * 256 + 128:tt * 256 + 256]
                sigmoid_act(zt[:, t, :], zp, [psB[b]], [Bzt])
                DVE("tensor_tensor", [psB[b], Bzt], [Bzt], out=zt[:, t, :], in0=zp, in1=zt[:, t, :], op=ALU.mult)
                POOL("tensor_tensor", [Bzt, Bdv], [Bgw], out=gw[:, t, :], in0=zt[:, t, :], in1=dn_wp, op=ALU.mult)

        if stage == 2:
            return finish()
        bsc = gbank()
        PE("matmul", [Btk, Bc], [psB[bsc]], out=ps[bsc][:, 0:4], lhsT=triU_f, rhs=tk[:, 0:4], start=True, stop=True, skip_group_check=True)
        PE("matmul", [Btk, Bc], [psB[bsc]], out=ps[bsc][:, 4:8], lhsT=ones_f, rhs=tk[:, 0:4], start=True, stop=True, skip_group_check=True)
        DVE("tensor_copy", [psB[bsc]], [Btk], out=tk[:, 8:12], in_=ps[bsc][:, 0:4])
        DVE("tensor_tensor", [psB[bsc], Btk], [Btk], out=tk[:, 12:16], in0=ps[bsc][:, 4:8], in1=tk[:, 8:12], op=ALU.subtract)
        ACT("activation", [Btk], [Btk], out=tk[:, 16:20], in_=tk[:, 8:12], func=AF.Exp)
        ACT("activation", [Btk], [Btk], out=tk[:, 20:24], in_=tk[:, 12:16], func=AF.Exp)
        ACT("activation", [psB[bsc]], [Btk], out=tk[:, 24:28], in_=ps[bsc][:, 4:8], func=AF.Exp)

        if stage == 21:
            return finish()
        for c in range(4):
            g = G[c % 2]
            kTc = kq[:, c, 0:128]
            qTc = kq[:, c, 128:256]
            gcol = tk[:, c:c + 1]
            beta = tk[:, 4 + c:5 + c]
            egc = tk[:, 16 + c:17 + c]
            edec = tk[:, 20 + c:21 + c]
            eglast = tk[:, 24 + c:25 + c]
            negbeta = tk[:, 28 + c:29 + c]
            b = gbank()
            pv = ps[b][:, :].bitcast(BF16)
            PE("transpose", [Bkq, Bcb], [psB[b]], out=pv[:, 0:128], in_=kTc, identity=ident_b)
            PE("transpose", [BvTd, Bcb], [psB[b]], out=pv[:, 128:256], in_=vTd[:, c * 128:(c + 1) * 128], identity=ident_b)
            ACT("activation", [psB[b], Btk], [g["Bke"]], out=g["ke"][:, :], in_=pv[:, 0:128], func=AF.Copy, scale=egc)
            tsmul(DVE, [psB[b], Btk], [g["Bkdec"]], g["kdec"][:, :], pv[:, 0:128], edec)
            ACT("activation", [psB[b]], [g["Bvt"]], out=g["vt"][:, :], in_=pv[:, 128:256], func=AF.Copy)
            if stage == 22:
                return finish()
            bk = gbank()
            PE("matmul", [Bkq], [psB[bk]], out=ps[bk][:, 0:256], lhsT=kTc, rhs=kq[:, c, :], start=True, stop=True)
            tsmul(DVE, [Bc, Btk], [g["BgTri"]], g["gTri"][:, :], triU_f, gcol)
            bd = gbank()
            PE("matmul", [Bc, g["BgTri"]], [psB[bd]], out=ps[bd][:, 0:128], lhsT=ones_f, rhs=g["gTri"][:, :], start=True, stop=False)
            PE("matmul", [Bc, g["BgTri"]], [psB[bd]], out=ps[bd][:, 0:128], lhsT=g["gTri"][:, :], rhs=negones_f, start=False, stop=False)
            PE("matmul", [Bc], [psB[bd]], out=ps[bd][:, 0:128], lhsT=ident_f, rhs=NEG_f, start=False, stop=True)
            ACT("activation", [psB[bd]], [g["BdecT"]], out=g["decT"][:, :], in_=ps[bd][:, 0:128], func=AF.Exp)
            POOL("tensor_tensor", [g["BdecT"], Bc], [g["BdecS"]], out=g["decS"][:, :], in0=g["decT"][:, :], in1=strictU_f, op=ALU.mult)
            DVE("scalar_tensor_tensor", [psB[bk], Btk, g["BdecS"]], [g["BNn"]], out=g["Nn"][:, :], in0=ps[bk][:, 0:128], scalar=negbeta, in1=g["decS"][:, :],
                op0=ALU.mult, op1=ALU.mult)
            DVE("tensor_tensor", [psB[bk], g["BdecT"]], [g["Bqkm"]], out=g["qkm"][:, :], in0=ps[bk][:, 128:256], in1=g["decT"][:, :], op=ALU.mult)
            if stage == 23:
                return finish()
            bt = gbank()
            ptv = ps[bt][:, :].bitcast(BF16)
            PE("transpose", [g["BNn"], Bcb], [psB[bt]], out=ptv[:, 0:128], in_=g["Nn"][:, :], identity=ident_b)
            DVE("tensor_copy", [psB[bt]], [g["BNT"]], out=g["NT"][:, :], in_=ptv[:, 0:128])
            DVE("tensor_tensor", [g["BNn"], Bcb], [g["BX0"]], out=g["X0"][:, :], in0=g["Nn"][:, :], in1=ident_b, op=ALU.add)
            P_ap, PT_ap, BP_ = g["Nn"][:, :], g["NT"][:, :], [g["BNn"], g["BNT"]]
            Xc, BXc = g["X0"], g["BX0"]
            for lvl in range(6):
                Pn, BPn = (g["P0"], g["BP0"]) if lvl % 2 == 0 else (g["P1"], g["BP1"])
                Xn, BXn = (g["X1"], g["BX1"]) if lvl % 2 == 0 else (g["X0"], g["BX0"])
                bp = gbank()
                PE("matmul", BP_, [psB[bp]], out=ps[bp][:, 0:128], lhsT=PT_ap, rhs=P_ap, start=True, stop=True, skip_group_check=True)
                PE("matmul", BP_, [psB[bp]], out=ps[bp][:, 128:256], lhsT=P_ap, rhs=PT_ap, start=True, stop=True, skip_group_check=True)
                if lvl % 2 == 0:
                    ACT("activation", [psB[bp]], [BPn], out=Pn[:, :], in_=ps[bp][:, 0:256], func=AF.Copy)
                else:
                    DVE("tensor_copy", [psB[bp]], [BPn], out=Pn[:, :], in_=ps[bp][:, 0:256])
                bx = gbank()
                PE("matmul", [Bcb, BXc], [psB[bx]], out=ps[bx][:, 0:128], lhsT=ident_b, rhs=Xc[:, :], start=True, stop=False)
                PE("matmul", [BPn, BXc], [psB[bx]], out=ps[bx][:, 0:128], lhsT=Pn[:, 128:256], rhs=Xc[:, :], start=False, stop=True)
                if lvl % 2 == 0:
                    DVE("tensor_copy", [psB[bx]], [BXn], out=Xn[:, :], in_=ps[bx][:, 0:128])
                else:
                    ACT("activation", [psB[bx]], [BXn], out=Xn[:, :], in_=ps[bx][:, 0:128], func=AF.Copy)
                P_ap, PT_ap, BP_ = Pn[:, 0:128], Pn[:, 128:256], [BPn]
                Xc, BXc = Xn, BXn
            if stage == 24:
                return finish()
            Xf, BXf = Xc, BXc
            bu = gbank()
            PE("matmul", [BXf, g["Bvt"]], [psB[bu]], out=ps[bu][:, 0:128], lhsT=Xf[:, :], rhs=g["vt"][:, :], start=True, stop=True, skip_group_check=True)
            PE("matmul", [BXf, g["Bke"]], [psB[bu]], out=ps[bu][:, 128:256], lhsT=g["ke"][:, :], rhs=Xf[:, :], start=True, stop=True, skip_group_check=True)
            ACT("activation", [psB[bu], Btk], [g["Bub"]], out=g["ub"][:, :], in_=ps[bu][:, 0:128], func=AF.Copy, scale=beta)
            DVE("tensor_copy", [psB[bu]], [g["BwT"]], out=g["wT"][:, :], in_=ps[bu][:, 128:256])
            if stage == 25:
                return finish()
            ba = gbank()
            PE("matmul", [g["BwT"], BSbf], [psB[ba]], out=ps[ba][:, 0:128], lhsT=g["wT"][:, :], rhs=Sbf[:, :], start=True, stop=True, skip_group_check=True)
            PE("matmul", [Bkq, BSbf], [psB[ba]], out=ps[ba][:, 128:256], lhsT=qTc, rhs=Sbf[:, :], start=True, stop=True, skip_group_check=True)
            DVE("scalar_tensor_tensor", [psB[ba], Btk, g["Bub"]], [g["Bvnew"]], out=g["vnew"][:, :], in0=ps[ba][:, 0:128], scalar=negbeta, in1=g["ub"][:, :],
                op0=ALU.mult, op1=ALU.add)
            ACT("activation", [psB[ba], Btk], [g["BBs"]], out=g["Bs"][:, :], in_=ps[ba][:, 128:256], func=AF.Copy, scale=egc)
            bs_ = gbank()
            PE("matmul", [g["Bkdec"], g["Bvnew"]], [psB[bs_]], out=ps[bs_][:, 0:128], lhsT=g["kdec"][:, :], rhs=g["vnew"][:, :], start=True, stop=True, skip_group_check=True)
            PE("matmul", [g["Bqkm"], g["Bvnew"]], [psB[bs_]], out=ps[bs_][:, 128:256], lhsT=g["qkm"][:, :], rhs=g["vnew"][:, :], start=True, stop=True, skip_group_check=True)
            DVE("scalar_tensor_tensor", [psB[bs_], Btk, BSst], [BSst], out=Sst[:, :], in0=Sst[:, :], scalar=eglast, in1=ps[bs_][:, 0:128], op0=ALU.mult, op1=ALU.add)
            ACT("activation", [BSst], [BSbf], out=Sbf[:, :], in_=Sst[:, :], func=AF.Copy)
            DVE("tensor_tensor", [psB[bs_], g["BBs"]], [g["Bog"]], out=g["og"][:, :], in0=ps[bs_][:, 128:256], in1=g["Bs"][:, :], op=ALU.add)
            ACT("activation", [g["Bog"]], [g["BBs"], g["BstG"]], out=g["Bs"][:, :], in_=g["og"][:, :], func=AF.Square, accum_out=g["stG"][:, 0:1])
            rsqrt_act(g["stG"][:, 1:2], g["stG"][:, 0:1], EPS_O, [g["BstG"]], [g["BstG"]])
            DVE("scalar_tensor_tensor", [g["Bog"], g["BstG"], Bgw], [Bmixo[ms]], out=mixo[ms][:, c, 128:256], in0=g["og"][:, :], scalar=g["stG"][:, 1:2], in1=gw[:, c, :],
                op0=ALU.mult, op1=ALU.mult)

        if stage == 3:
            return finish()
        def Oreg(t, m):
            gi = t * 2 + m
            if gi < 3:
                return ps[0][:, gi * 129:(gi + 1) * 129], psB[0]
            return ps[1][:, 0:129], psB[1]

        for hh in range(2):
            H = blk * 2 + hh
            DVE("memset", [], [psB[0]], ap=ps[0][:, 0:387], constant=0.0)
            DVE("memset", [], [psB[1]], ap=ps[1][:, 0:129], constant=0.0)
            for j in range(2 * H + 2):
                t_lo = 0 if j <= 2 * H else 1
                width = (2 - t_lo) * 128
                qc0 = hh * 256 + t_lo * 128
                sb = 2 + (stcnt[0] % 2)
                sl = stcnt[0] % 2
                stcnt[0] += 1
                for m in range(2):
                    PE("matmul", [BKT[j // 4], BQT], [psB[sb]], out=ps[sb][:, m * 256 + t_lo * 128:m * 256 + 256], lhsT=KT[:, j * 128:(j + 1) * 128],
                       rhs=QTp[m][:, qc0:qc0 + width], start=True, stop=True, skip_group_check=True)
                if t_lo == 0:
                    ACT("activation", [psB[sb]], [BPT[sl]], out=PT[sl][:, :], in_=ps[sb][:, :], func=AF.Exp)
                else:
                    ACT("activation", [psB[sb]], [BPT[sl]], out=PT[sl][:, :].rearrange("p (m q) -> p m q", m=2)[:, :, 128:256],
                        in_=ps[sb][:, :].rearrange("p (m q) -> p m q", m=2)[:, :, 128:256], func=AF.Exp)
                if j >= 2 * H:
                    td = j - 2 * H
                    for m in range(2):
                        reg = PT[sl][:, m * 256 + td * 128:m * 256 + td * 128 + 128]
                        POOL("tensor_tensor", [BPT[sl], Bcb], [BPT[sl]], out=reg, in0=reg, in1=triU_b, op=ALU.mult)
                for t in range(t_lo, 2):
                    for m in range(2):
                        oap, oB = Oreg(t, m)
                        PE("matmul", [BPT[sl], BV[j // 4]], [oB], out=oap, lhsT=PT[sl][:, m * 256 + t * 128:m * 256 + t * 128 + 128], rhs=Vaug[:, j, 0:129],
                           start=False, stop=False, skip_group_check=True)
            for t in range(2):
                o0, B0 = Oreg(t, 0)
                o1, B1 = Oreg(t, 1)
                tl = hh * 2 + t
                DVE("reciprocal", [B0], [BstA], out=stA[:, 0:1], in_=o0[:, 128:129])
                DVE("reciprocal", [B1], [BstA], out=stA[:, 1:2], in_=o1[:, 128:129])
                DVE("tensor_tensor", [BstA, Bdv], [BstA], out=stA[:, 2:3], in0=stA[:, 1:2], in1=neglam, op=ALU.mult)
                ACT("activation", [B0, BstA], [Bot], out=ot[:, :], in_=o0[:, 0:128], func=AF.Copy, scale=stA[:, 0:1])
                DVE("scalar_tensor_tensor", [B1, BstA, Bot], [Boa], out=oa[:, :], in0=o1[:, 0:128], scalar=stA[:, 2:3], in1=ot[:, :], op0=ALU.mult, op1=ALU.add)
                ACT("activation", [Boa], [Bot, BstA], out=ot[:, :], in_=oa[:, :], func=AF.Square, accum_out=stA[:, 3:4])
                rsqrt_act(stA[:, 4:5], stA[:, 3:4], EPS_O, [BstA], [BstA])
                DVE("scalar_tensor_tensor", [Boa, BstA, Bdv], [Bmixo[ms]], out=mixo[ms][:, tl, 0:128], in0=oa[:, :], scalar=stA[:, 4:5], in1=da_wp, op0=ALU.mult, op1=ALU.mult)

        tok0 = blk * 512
        r_ = tok0 // TC
        i_ = (tok0 % TC) // PS
        row0 = r_ * PS + (tok0 % TC - i_ * PS)
        dst = mix_src[i_].ap()[row0:row0 + 512, :].rearrange("(t p) c -> p t c", p=128)
        DMA("pool", f"mixst{ms}", dst, mixo[ms][:, :, :], [Bmixo[ms]], [BMIX[i_]])

    if stage == 4:
        return finish()
    BALL = [Buf(f"mix_all{i}") for i in range(NCOL)]
    for i in range(NCOL):
        sc.op("pool", lambda e, i=i: e.collective_compute("AllGather", ALU.bypass, replica_groups=[[0, 1, 2, 3], [4, 5, 6, 7]],
                                                          ins=[mix_src[i].ap().opt()], outs=[mix_all[i].ap().opt()]),
              [BMIX[i]], [BALL[i]], dma=f"cc{i}", inc=1)
    if debug and NCOL == 1:
        DMA("pool", "dbg", dbg_d[:, :], mix_src[0].ap()[:, :], BMIX, [Buf("dbg")])

    if stage == 5:
        return finish()
    sc.barrier()
    A.off = mark_common
    glo[0] = 0
    WoB = A.alloc("WoB", [128, 8, D], BF16)
    WuB = A.alloc("WuB", [128, 8, 4 * D], BF16)
    WdB = A.alloc("WdB", [128, 32, D], BF16)
    BWo, BWu, BWd = Buf("WoB"), Buf("WuB"), Buf("WdB")
    wsg = [A.alloc(f"wsg{i}", [128, 1024], F32) for i in range(2)]
    Bwsg = [Buf(f"wsg{i}") for i in range(2)]
    xr = [A.alloc(f"xr{i}", [128, 2, D], F32) for i in range(2)]
    Bxr = [Buf(f"xr{i}") for i in range(2)]
    mtm = A.alloc("mtm", [128, 2, D], BF16)
    Bmtm = Buf("mtm")
    mT = A.alloc("mT", [128, 8, CB], BF16)
    BmT = Buf("mT")
    aT = A.alloc("aT", [128, 32, CB], BF16)
    BaT = Buf("aT")
    rl = [A.alloc(f"rl{i}", [128, CB], F32) for i in range(2)]
    Brl = [Buf(f"rl{i}") for i in range(2)]
    stC = A.alloc("stC", [128, 8], F32)
    BstC = Buf("stC")

    dyn = {}

    def load_off(e):
        reg = e.alloc_register("tokoff")
        ins = e.reg_load(reg, off_d[0:1, 0:1])
        dyn["v"] = e.snap(reg, min_val=0, max_val=3 * PS)
        return ins

    sc.op("pool", load_off, [], [])

    wcnt = [0]
    cast_engs = (DVE, POOL, ACT)

    def wload(dst_ap, src_ap, width, Bdst, scale_ap=None, shape3=None):
        i = wcnt[0] % 2
        eng = cast_engs[wcnt[0] % 3]
        wcnt[0] += 1
        stg = wsg[i][:, 0:width]
        if shape3 is not None:
            stg = stg.rearrange("p (f n) -> p f n", f=shape3)
        DMA("sp", f"wsg{i}", stg, src_ap, [], [Bwsg[i]])
        if scale_ap is not None:
            if eng is ACT:
                ACT("activation", [Bwsg[i], Bdv], [Bdst], out=dst_ap, in_=stg, func=AF.Copy, scale=scale_ap)
            else:
                tsmul(eng, [Bwsg[i], Bdv], [Bdst], dst_ap, stg, scale_ap)
        else:
            if eng is ACT:
                ACT("activation", [Bwsg[i]], [Bdst], out=dst_ap, in_=stg, func=AF.Copy)
            else:
                eng("tensor_copy", [Bwsg[i]], [Bdst], out=dst_ap, in_=stg)

    for kc in range(8):
        wload(WoB[:, kc, :], wout_d[kc * 128:(kc + 1) * 128, :], 1024, BWo)
    for kc in range(8):
        for qf in range(4):
            wload(WuB[:, kc, qf * 1024:(qf + 1) * 1024], wup_d[kc * 128:(kc + 1) * 128, qf * 1024:(qf + 1) * 1024], 1024, BWu, scale_ap=n2s[:, kc:kc + 1])
    for fc in range(32):
        wload(WdB[:, fc, :], wdn_d[fc * 128:(fc + 1) * 128, :], 1024, BWd)

    BMINE = Buf("mix_mine")
    for i in range(NCOL):
        for h4 in range(4):
            def mcopy(e, h4=h4, i=i):
                return e.dma_start(out=mix_mine.ap()[h4 * TC + i * PS:h4 * TC + (i + 1) * PS, :], in_=mix_all[i].ap()[bass.ds(dyn["v"] + h4 * 4 * PS, PS), :])
            sc.op("pool", mcopy, [BALL[i]], [BMINE], dma="mmine")
    mview = mix_mine.ap().rearrange("(h t) c -> t h c", h=4)
    for cb in range(NCB):
        xs = cb % 2
        xsrc = xres_d[cb * CB:(cb + 1) * CB, :].rearrange("(t p) d -> p t d", p=128)
        for tt in range(2):
            DMA("sp", f"xr{xs}", xr[xs][:, tt, :], xsrc[:, tt, :], [], [Bxr[xs]])
        for tt in range(2):
            DMA("sp", "mtm", mtm[:, tt, :].rearrange("p (h c) -> p h c", h=4), mview[cb * CB + tt * 128:cb * CB + (tt + 1) * 128, :, :], [BMINE], [Bmtm])
        for kc in range(8):
            b = gbank()
            pv = ps[b][:, :].bitcast(BF16)
            for tt in range(2):
                PE("transpose", [Bmtm, Bcb], [psB[b]], out=pv[:, tt * 128:(tt + 1) * 128], in_=mtm[:, tt, kc * 128:(kc + 1) * 128], identity=ident_b)
            if kc % 2 == 0:
                ACT("activation", [psB[b]], [BmT], out=mT[:, kc, :], in_=pv[:, 0:CB], func=AF.Copy)
            else:
                DVE("tensor_copy", [psB[b]], [BmT], out=mT[:, kc, :], in_=pv[:, 0:CB])
        for tt in range(2):
            for n in range(2):
                b = gbank()
                for kc in range(8):
                    PE("matmul", [BmT, BWo], [psB[b]], out=ps[b][:, :], lhsT=mT[:, kc, tt * 128:(tt + 1) * 128], rhs=WoB[:, kc, n * 512:(n + 1) * 512], start=(kc == 0), stop=(kc == 7))
                reg = xr[xs][:, tt, n * 512:(n + 1) * 512]
                DVE("tensor_tensor", [psB[b], Bxr[xs]], [Bxr[xs]], out=reg, in0=ps[b][:, :], in1=reg, op=ALU.add)
        for tt in range(2):
            ACT("activation", [Bxr[xs]], [Bmtm, BstC], out=mtm[:, tt, :], in_=xr[xs][:, tt, :], func=AF.Square, accum_out=stC[:, tt:tt + 1])
        rsqrt_act(stC[:, 2:4], stC[:, 0:2], EPS_X, [BstC], [BstC])
        for tt in range(2):
            tsmul(DVE if tt == 0 else POOL, [Bxr[xs], BstC], [Bmtm], mtm[:, tt, :], xr[xs][:, tt, :], stC[:, 2 + tt:3 + tt])
        for kc in range(8):
            b = gbank()
            pv = ps[b][:, :].bitcast(BF16)
            for tt in range(2):
                PE("transpose", [Bmtm, Bcb], [psB[b]], out=pv[:, tt * 128:(tt + 1) * 128], in_=mtm[:, tt, kc * 128:(kc + 1) * 128], identity=ident_b)
            if kc % 2 == 0:
                ACT("activation", [psB[b]], [BmT], out=mT[:, kc, :], in_=pv[:, 0:CB], func=AF.Copy)
            else:
                DVE("tensor_copy", [psB[b]], [BmT], out=mT[:, kc, :], in_=pv[:, 0:CB])
        for fc in range(32):
            b = gbank()
            for kc in range(8):
                PE("matmul", [BmT, BWu], [psB[b]], out=ps[b][:, 0:CB], lhsT=WuB[:, kc, fc * 128:(fc + 1) * 128], rhs=mT[:, kc, :], start=(kc == 0), stop=(kc == 7))
            ri = fc % 2
            ACT("activation", [psB[b]], [Brl[ri]], out=rl[ri][:, :], in_=ps[b][:, 0:CB], func=AF.Relu)
            (DVE if fc % 2 == 0 else POOL)("tensor_tensor", [Brl[ri]], [BaT], out=aT[:, fc, :], in0=rl[ri][:, :], in1=rl[ri][:, :], op=ALU.mult)
        for tt in range(2):
            for n in range(2):
                b = gbank()
                for fc in range(32):
                    PE("matmul", [BaT, BWd], [psB[b]], out=ps[b][:, :], lhsT=aT[:, fc, tt * 128:(tt + 1) * 128], rhs=WdB[:, fc, n * 512:(n + 1) * 512], start=(fc == 0), stop=(fc == 31))
                reg = xr[xs][:, tt, n * 512:(n + 1) * 512]
                DVE("tensor_tensor", [psB[b], Bxr[xs]], [Bxr[xs]], out=reg, in0=ps[b][:, :], in1=reg, op=ALU.add)
        ydst = y_d[cb * CB:(cb + 1) * CB, :].rearrange("(t p) d -> p t d", p=128)
        for tt in range(2):
            DMA("sp", f"yst{xs}", ydst[:, tt, :], xr[xs][:, tt, :], [Bxr[xs]], [Buf("y")])
    sc.barrier()

    return finish()


def _consts():
    j = np.arange(128)[:, None]
    i = np.arange(128)[None, :]
    ident = (i == j).astype(np.float32)
    triU = (j <= i).astype(np.float32)
    ones = np.ones((128, 128), np.float32)
    neg = np.where(i >= j, 0.0, NEGBIG).astype(np.float32)
    strictU = (i > j).astype(np.float32)
    blk = ((i // 64) == (j // 64)).astype(np.float32)
    return np.concatenate([ident, triU, ones, -ones, neg, strictU, blk], axis=1)


def make_in_maps(inp, S):
    x = np.asarray(inp["x"], np.float32)
    w_in = np.asarray(inp["w_in"], np.float32)[0]
    TC = S // 4
    cst = _consts()
    w_out = np.asarray(inp["w_out"], np.float32)[0]
    rows = np.concatenate([np.concatenate([np.arange(h * 128, (h + 1) * 128), 512 + np.arange(h * 128, (h + 1) * 128)]) for h in range(4)])
    w_out_p = np.ascontiguousarray(w_out[rows])
    w_up = np.ascontiguousarray(np.asarray(inp["w_up"], np.float32)[0])
    w_dn = np.ascontiguousarray(np.asarray(inp["w_down"], np.float32)[0])
    rep = lambda v: np.broadcast_to(np.asarray(v, np.float32).reshape(1, -1), (128, np.asarray(v).size))
    maps = []
    for core in range(8):
        b, h = core // 4, core % 4
        cols = np.concatenate([
            np.arange(h * 128, (h + 1) * 128),
            512 + np.arange(h * 128, (h + 1) * 128),
            1536 + np.arange(h * 128, (h + 1) * 128),
            2048 + np.arange(h * 128, (h + 1) * 128),
            2560 + np.arange(h * 128, (h + 1) * 128),
            1024 + np.arange(h * 128, (h + 1) * 128),
            3072 + np.arange(h * 128, (h + 1) * 128),
            np.array([3584 + h, 3588 + h]),
        ])
        w_c = np.ascontiguousarray(w_in[:, cols])
        cw = np.asarray(inp["conv_w"], np.float32)[0]
        taps = np.concatenate([cw[:, g * 512 + h * 128:g * 512 + (h + 1) * 128].T for g in range(3)], axis=1)
        pp = np.concatenate([
            np.asarray(inp["norm1_w"], np.float32)[0].reshape(8, 128).T,
            np.asarray(inp["norm2_w"], np.float32)[0].reshape(8, 128).T,
            np.tile(np.asarray(inp["q_norm_w"], np.float32)[0], 2).reshape(128, 1),
            np.tile(np.asarray(inp["k_norm_w"], np.float32)[0], 2).reshape(128, 1),
            taps,
            np.full((128, 1), np.asarray(inp["A_log"], np.float32)[0, h], np.float32),
            np.full((128, 1), np.asarray(inp["dt_bias"], np.float32)[0, h], np.float32),
            rep(inp["lambda_q1"][0]), rep(inp["lambda_k1"][0]), rep(inp["lambda_q2"][0]), rep(inp["lambda_k2"][0]),
            rep(inp["da_out_norm_w"][0]), rep(inp["dn_out_norm_w"][0]),
        ], axis=1).astype(np.float32)
        assert pp.shape == (128, 544), pp.shape
        maps.append({
            "x": np.ascontiguousarray(x[b, :S]),
            "xres": np.ascontiguousarray(x[b, h * TC:(h + 1) * TC]),
            "w_in": w_c,
            "consts": cst,
            "pp": np.ascontiguousarray(pp),
            "w_out": w_out_p,
            "w_up": w_up,
            "w_down": w_dn,
            "tokoff": np.array([[h * (TC // max(1, TC // 512))]], np.int32),
        })
    return maps


_CACHE = {}


def kernel(**inputs):
    S = 8192
    if "nc" not in _CACHE:
        _CACHE["nc"] = build_program(S)[0]
    nc = _CACHE["nc"]
    maps = make_in_maps(inputs, S)
    res = run_bass_kernel_spmd(nc, maps, core_ids=list(range(8)))
    TC = S // 4
    out = np.empty((2, S, D), np.float32)
    for core in range(8):
        b, r = core // 4, core % 4
        out[b, r * TC:(r + 1) * TC] = np.asarray(res.results[core]["y"], np.float32)
    return out
```

```python
import numpy as np
import ml_dtypes
import concourse.bass as bass
import concourse.mybir as mybir
from concourse.bass_utils import run_bass_kernel_spmd

F32 = mybir.dt.float32
BF16 = mybir.dt.bfloat16
I32 = mybir.dt.int32
AF = mybir.ActivationFunctionType
ALU = mybir.AluOpType
AX = mybir.AxisListType

D = 1024
EPS = 1e-6
LAM_INIT = 0.2
NEGBIG = -30000.0
ENG = ("pe", "act", "dve", "pool", "sp")


class Buf:
    __slots__ = ("name", "w", "r", "excl")

    def __init__(self, name, excl=False):
        self.name = name
        self.w = None
        self.r = {}
        self.excl = excl


class Op:
    __slots__ = ("eng", "fn", "deps", "signal", "sigval", "dma", "chanval", "idx", "inc")


class Sched:
    def __init__(self):
        self.q = {e: [] for e in ENG}
        self.chans = {}
        self.n = 0

    def barrier(self):
        lasts = []
        for e in ENG:
            comp = [o for o in self.q[e] if o.dma is None and o.fn is not None]
            if comp:
                comp[-1].signal = True
                lasts.append(comp[-1])
        seen = {}
        for e in ENG:
            for o in self.q[e]:
                if o.dma is not None:
                    seen[o.dma] = o
        lasts += list(seen.values())
        for e in ENG:
            o = Op()
            o.eng = e
            o.fn = None
            o.signal = False
            o.sigval = 0
            o.dma = None
            o.chanval = 0
            o.inc = 0
            o.idx = self.n
            self.n += 1
            o.deps = [d for d in lasts if not (d.dma is None and d.eng == e)]
            self.q[e].append(o)

    def op(self, eng, fn, r=(), w=(), dma=None, inc=16):
        o = Op()
        o.inc = inc
        o.eng = eng
        o.fn = fn
        o.signal = False
        o.sigval = 0
        o.dma = dma
        o.chanval = 0
        o.idx = self.n
        self.n += 1
        if dma is not None:
            c = self.chans.setdefault(dma, [0])
            c[0] += 1
            c[0] += 0
            o.chanval = inc * c[0]
        deps = {}

        def add(d, raw):
            if d is None or d is o:
                return
            if d.dma is None and d.eng == eng and dma is None:
                if eng == "pe" or not raw:
                    return
            key = ("d", d.dma) if d.dma is not None else ("e", d.eng)
            cur = deps.get(key)
            if cur is None or cur.idx < d.idx:
                deps[key] = d

        for b in r:
            add(b.w, True)
            if b.excl:
                for x in b.r.values():
                    add(x, False)
        for b in w:
            add(b.w, False)
            for x in b.r.values():
                add(x, False)
        o.deps = list(deps.values())
        for d in o.deps:
            d.signal = True
        for b in r:
            key = ("d", dma, o.idx) if dma is not None else eng
            b.r[key] = o
        for b in w:
            b.w = o
            b.r = {}
        self.q[eng].append(o)
        return o

    def emit(self, nc, block, sems, chan_sems):
        for e in ENG:
            n = 0
            for o in self.q[e]:
                if o.dma is None and o.signal:
                    n += 1
                    o.sigval = n

        def run(engname):
            def body(eobj):
                waited = {}
                for o in self.q[engname]:
                    for d in o.deps:
                        if d.dma is not None:
                            key = ("d", d.dma)
                            sem = chan_sems[d.dma]
                            val = d.chanval
                        else:
                            key = ("e", d.eng)
                            sem = sems[d.eng]
                            val = d.sigval
                        if waited.get(key, 0) >= val:
                            continue
                        eobj.wait_ge(sem, val)
                        waited[key] = val
                    if o.fn is None:
                        continue
                    ins = o.fn(eobj)
                    if o.dma is not None:
                        ins.then_inc(chan_sems[o.dma], o.inc)
                    elif o.signal:
                        ins.then_inc(sems[engname], 1)
            return body

        block.tensor(run("pe"))
        block.scalar(run("act"))
        block.vector(run("dve"))
        block.gpsimd(run("pool"))
        block.sync(run("sp"))


class Arena:
    def __init__(self, nc, base=16640, limit=229376):
        self.nc = nc
        self.off = base
        self.limit = limit
        self.k = 0

    def alloc(self, name, shape, dtype):
        size = 2 if dtype == BF16 else 4
        n = 1
        for s in shape[1:]:
            n *= s
        nbytes = n * size
        self.off = (self.off + 63) // 64 * 64
        assert self.off + nbytes <= self.limit, f"SBUF overflow at {name}: {self.off + nbytes}"
        self.k += 1
        t = self.nc.alloc_sbuf_tensor_at(f"{name}_{self.k}", list(shape), dtype, offset=self.off)
        self.off += nbytes
        return t


def build_program(S=8192, debug=False, stage=99):
    nc = bass.Bass("TRN2", target_bir_lowering=False)
    NBLK = S // 512
    NT = S // 128
    TC = S // 4
    CB = 256
    NCB = TC // CB

    x_d = nc.dram_tensor("x", [S, D], F32, kind="ExternalInput").ap()
    xres_d = nc.dram_tensor("xres", [TC, D], F32, kind="ExternalInput").ap()
    win_d = nc.dram_tensor("w_in", [D, 898], F32, kind="ExternalInput").ap()
    cst_d = nc.dram_tensor("consts", [128, 896], F32, kind="ExternalInput").ap()
    pp_d = nc.dram_tensor("pp", [128, 544], F32, kind="ExternalInput").ap()
    wout_d = nc.dram_tensor("w_out", [D, D], F32, kind="ExternalInput").ap()
    wup_d = nc.dram_tensor("w_up", [D, 4 * D], F32, kind="ExternalInput").ap()
    wdn_d = nc.dram_tensor("w_down", [4 * D, D], F32, kind="ExternalInput").ap()
    off_d = nc.dram_tensor("tokoff", [1, 1], I32, kind="ExternalInput").ap()
    y_d = nc.dram_tensor("y", [TC, D], F32, kind="ExternalOutput").ap()
    NCOL = max(1, TC // 512)
    PS = TC // NCOL
    mix_src = [nc.dram_tensor(f"mix_src{i}", [4 * PS, 256], BF16) for i in range(NCOL)]
    mix_all = [nc.dram_tensor(f"mix_all{i}", [16 * PS, 256], BF16) for i in range(NCOL)]
    mix_mine = nc.dram_tensor("mix_mine", [4 * (S // 4), 256], BF16)
    if debug:
        dbg_d = nc.dram_tensor("dbg", [S, 256], BF16, kind="ExternalOutput").ap()

    sc = Sched()
    A = Arena(nc)


    def finish():
        sc.barrier()
        from contextlib import ExitStack
        with ExitStack() as es:
            sems = {e: es.enter_context(nc.semaphore(f"sem_{e}")) for e in ENG}
            chan_sems = {c: es.enter_context(nc.semaphore(f"ch_{c}")) for c in sc.chans}
            block = es.enter_context(nc.Block())
            sc.emit(nc, block, sems, chan_sems)
        info = dict(n_ops=sc.n, per_eng={e: len(sc.q[e]) for e in ENG}, off=A.off)
        return nc, info

    def mk(eng):
        def f(method, r, w, **kw):
            return sc.op(eng, lambda e, m=method, kw=kw: getattr(e, m)(**kw), r, w)
        return f

    PE, ACT, DVE, POOL = mk("pe"), mk("act"), mk("dve"), mk("pool")

    def DMA(q, chan, out, in_, r, w):
        return sc.op(q, lambda e, o=out, i=in_: e.dma_start(out=o, in_=i), r, w, dma=chan)

    ps = [nc.alloc_psum_tensor(f"ps{i}", [128, 512], F32) for i in range(8)]
    psB = [Buf(f"ps{i}", excl=True) for i in range(8)]
    gctr = [0]
    glo = [4]

    def gbank():
        n = 8 - glo[0]
        i = glo[0] + (gctr[0] % n)
        gctr[0] += 1
        return i

    cst = A.alloc("cst", [128, 896], F32)
    cstb = A.alloc("cstb", [128, 896], BF16)
    pp = A.alloc("pp", [128, 544], F32)
    dv = A.alloc("dv", [128, 320], F32)
    epsT = A.alloc("epsT", [128, 8], F32)
    tmp64 = A.alloc("tmp64", [128, 64], F32)
    Bc, Bcb, Bpp, Bdv, Beps, Btmp64 = (Buf(n) for n in ("cst", "cstb", "pp", "dv", "eps", "tmp64"))

    DMA("sp", "cst", cst[:, :], cst_d[:, :], [], [Bc])
    DMA("sp", "pp", pp[:, :], pp_d[:, :], [], [Bpp])
    DVE("tensor_copy", [Bc], [Bcb], out=cstb[:, :], in_=cst[:, :])
    ident_f, triU_f, ones_f, negones_f, NEG_f, strictU_f = (cst[:, i * 128:(i + 1) * 128] for i in range(6))
    ident_b = cstb[:, 0:128]
    triU_b = cstb[:, 128:256]
    ones_b = cstb[:, 256:384]
    blk_b = cstb[:, 768:896]

    epsvals = [1024 * EPS, 64 * EPS, EPS, 128 * EPS, 1.0]
    for i, v in enumerate(epsvals):
        DVE("memset", [], [Beps], ap=epsT[:, i:i + 1], constant=v)
    EPS_X, EPS_QK, EPS_L2, EPS_O, ONE = (epsT[:, i:i + 1] for i in range(5))

    def tsmul(eng, r, w, out, in0, s):
        return eng("tensor_scalar", r, w, out=out, in0=in0, scalar1=s, scalar2=None, op0=ALU.mult)

    tsmul(DVE, [Bpp], [Bdv], dv[:, 0:16], pp[:, 0:16], 32.0)
    tsmul(DVE, [Bpp], [Bdv], dv[:, 16:17], pp[:, 17:18], 8.0)
    ACT("activation", [Bpp], [Bdv], out=dv[:, 25:26], in_=pp[:, 30:31], func=AF.Exp)
    tsmul(DVE, [Bdv], [Bdv], dv[:, 17:18], dv[:, 25:26], -1.0)
    DVE("tensor_tensor", [Bpp], [Btmp64], out=tmp64[:, :], in0=pp[:, 32:96], in1=pp[:, 96:160], op=ALU.mult)
    DVE("reduce_sum", [Btmp64], [Bdv], out=dv[:, 20:21], in_=tmp64[:, :], axis=AX.X)
    DVE("tensor_tensor", [Bpp, Bdv], [Btmp64], out=tmp64[:, :], in0=pp[:, 160:224], in1=pp[:, 224:288], op=ALU.mult)
    DVE("reduce_sum", [Btmp64], [Bdv], out=dv[:, 21:22], in_=tmp64[:, :], axis=AX.X)
    ACT("activation", [Bdv], [Bdv], out=dv[:, 22:24], in_=dv[:, 20:22], func=AF.Exp)
    DVE("tensor_tensor", [Bdv], [Bdv], out=dv[:, 24:25], in0=dv[:, 23:24], in1=dv[:, 22:23], op=ALU.subtract)
    DVE("tensor_scalar", [Bdv], [Bdv], out=dv[:, 18:19], in0=dv[:, 24:25], scalar1=-LAM_INIT, scalar2=None, op0=ALU.add)
    tsmul(DVE, [Bpp], [Bdv], dv[:, 64:192], pp[:, 288:416], float(np.sqrt(128.0) * (1.0 - LAM_INIT)))
    tsmul(DVE, [Bpp], [Bdv], dv[:, 192:320], pp[:, 416:544], float(np.sqrt(128.0)))
    n1s = dv[:, 0:8]
    n2s = dv[:, 8:16]
    wq = pp[:, 16:17]
    wk8 = dv[:, 16:17]
    negA = dv[:, 17:18]
    neglam = dv[:, 18:19]
    dtb = pp[:, 31:32]
    da_wp = dv[:, 64:192]
    dn_wp = dv[:, 192:320]

    mark_common = A.off
    if stage == 0:
        return finish()

    def rsqrt_act(out_ap, in_ap, eps_ap, r, w):
        ACT("activation", r + [Beps], w, out=out_ap, in_=in_ap, func=AF.Ln, bias=eps_ap, scale=1.0)
        ACT("activation", w, w, out=out_ap, in_=out_ap, func=AF.Exp, scale=-0.5)

    def sigmoid_act(out_ap, in_ap, r, w):
        ACT("activation", r, w, out=out_ap, in_=in_ap, func=AF.Exp, scale=-1.0)
        ACT("activation", w + [Beps], w, out=out_ap, in_=out_ap, func=AF.Ln, bias=ONE, scale=1.0)
        ACT("activation", w, w, out=out_ap, in_=out_ap, func=AF.Exp, scale=-1.0)

    Wb = A.alloc("Wb", [128, 8, 898], BF16)
    wst = [A.alloc(f"wst{i}", [128, 898], F32) for i in range(2)]
    Bwst = [Buf(f"wst{i}") for i in range(2)]
    BWb = Buf("Wb")
    for kc in range(8):
        s = kc % 2
        DMA("sp", f"wst{s}", wst[s][:, :], win_d[kc * 128:(kc + 1) * 128, :], [], [Bwst[s]])
        tsmul(DVE, [Bwst[s], Bdv], [BWb], Wb[:, kc, :], wst[s][:, :], n1s[:, kc:kc + 1])

    xt = [A.alloc(f"xt{i}", [128, 4, D], F32) for i in range(2)]
    Bxt = [Buf(f"xt{i}") for i in range(2)]
    junk = A.alloc("junk", [128, D], BF16)
    Bjunk = Buf("junk")
    xb = A.alloc("xb", [128, 4, D], BF16)
    Bxb = Buf("xb")
    hT = A.alloc("hT", [128, 8, 512], BF16)
    BhT = Buf("hT")
    stx = A.alloc("stx", [128, 16], F32)
    Bstx = Buf("stx")
    KT = A.alloc("KT", [128, S], BF16)
    BKT = [Buf(f"KT{i}") for i in range(NBLK)]
    Vaug = A.alloc("Vaug", [128, NT, 130], BF16)
    BV = [Buf(f"V{i}") for i in range(NBLK)]
    QTp = [A.alloc(f"QTp{m}", [128, 512], BF16) for m in range(2)]
    BQT = Buf("QTp")
    sqb = A.alloc("sqb", [128, 512], BF16)
    Bsqb = Buf("sqb")
    rr = A.alloc("rr", [128, 512], F32)
    Brr = Buf("rr")
    cbuf = [A.alloc(f"cbuf{i}", [128, 515], F32) for i in range(3)]
    Bcbf = [Buf(f"cbuf{i}") for i in range(3)]
    cy = [A.alloc(f"cy{i}", [128, 512], F32) for i in range(3)]
    Bcy = [Buf(f"cy{i}") for i in range(3)]
    sg = [A.alloc(f"sg{i}", [128, 512], F32) for i in range(3)]
    Bsg = [Buf(f"sg{i}") for i in range(3)]
    kq = A.alloc("kq", [128, 4, 256], BF16)
    Bkq = Buf("kq")
    vTd = A.alloc("vTd", [128, 512], BF16)
    BvTd = Buf("vTd")
    gw = A.alloc("gw", [128, 4, 128], F32)
    Bgw = Buf("gw")
    zt = A.alloc("zt", [128, 4, 128], F32)
    Bzt = Buf("zt")
    tk = A.alloc("tk", [128, 64], F32)
    Btk = Buf("tk")
    PT = [A.alloc(f"PT{i}", [128, 512], BF16) for i in range(2)]
    BPT = [Buf(f"PT{i}") for i in range(2)]
    mixo = [A.alloc(f"mixo{i}", [128, 4, 256], BF16) for i in range(2)]
    Bmixo = [Buf(f"mixo{i}") for i in range(2)]
    ot = A.alloc("ot", [128, 128], F32)
    oa = A.alloc("oa", [128, 128], F32)
    Bot, Boa = Buf("ot"), Buf("oa")
    stA = A.alloc("stA", [128, 8], F32)
    BstA = Buf("stA")
    Sst = A.alloc("Sst", [128, 128], F32)
    Sbf = A.alloc("Sbf", [128, 128], BF16)
    BSst, BSbf = Buf("Sst"), Buf("Sbf")
    BMIX = [Buf(f"mix_src{i}") for i in range(NCOL)]
    G = []
    for p in range(4):
        g = {}
        for nm, dt_, w_ in (("ke", BF16, 128), ("kdec", BF16, 128), ("vt", BF16, 128), ("gTri", F32, 128),
                            ("decT", F32, 128), ("decS", F32, 128), ("Nn", BF16, 128), ("NT", BF16, 128),
                            ("P0", BF16, 256), ("P1", BF16, 256), ("X0", BF16, 128), ("X1", BF16, 128),
                            ("qkm", BF16, 128), ("ub", F32, 128), ("wT", BF16, 128), ("vnew", BF16, 128),
                            ("Bs", F32, 128), ("og", F32, 128), ("stG", F32, 8)):
            g[nm] = A.alloc(f"{nm}{p}", [128, w_], dt_)
            g["B" + nm] = Buf(f"{nm}{p}")
        G.append(g)
    phaseA_end = A.off

    for i in range(3):
        POOL("memset", [], [Bcbf[i]], ap=cbuf[i][:, 0:3], constant=0.0)
    POOL("memset", [], BV, ap=Vaug[:, :, 128:129], constant=1.0)
    DVE("memset", [], [BSst], ap=Sst[:, :], constant=0.0)
    DVE("memset", [], [BSbf], ap=Sbf[:, :], constant=0.0)
    for m in range(2):
        DVE("memset", [], [BQT], ap=QTp[m][:, :], constant=0.0)

    def load_x(blk):
        s = blk % 2
        src = x_d[blk * 512:(blk + 1) * 512, :].rearrange("(t p) d -> p t d", p=128)
        for t in range(4):
            DMA("sp", f"xt{s}", xt[s][:, t, :], src[:, t, :], [], [Bxt[s]])

    load_x(0)
    stcnt = [0]

    for blk in range(NBLK):
        s = blk % 2
        ms = blk % 2
        if blk + 1 < NBLK:
            load_x(blk + 1)
        for t in range(4):
            ACT("activation", [Bxt[s]], [Bjunk, Bstx], out=junk[:, :], in_=xt[s][:, t, :], func=AF.Square, accum_out=stx[:, t:t + 1])
        rsqrt_act(stx[:, 4:8], stx[:, 0:4], EPS_X, [Bstx], [Bstx])
        for t in range(4):
            if t % 2 == 0:
                tsmul(DVE, [Bxt[s], Bstx], [Bxb], xb[:, t, :], xt[s][:, t, :], stx[:, 4 + t:5 + t])
            else:
                ACT("activation", [Bxt[s], Bstx], [Bxb], out=xb[:, t, :], in_=xt[s][:, t, :], func=AF.Copy, scale=stx[:, 4 + t:5 + t])
        for kc in range(8):
            b = gbank()
            pv = ps[b][:, :].bitcast(BF16)
            for t in range(4):
                PE("transpose", [Bxb, Bcb], [psB[b]], out=pv[:, t * 128:(t + 1) * 128], in_=xb[:, t, kc * 128:(kc + 1) * 128], identity=ident_b)
            if kc % 2 == 0:
                ACT("activation", [psB[b]], [BhT], out=hT[:, kc, :], in_=pv[:, 0:512], func=AF.Copy)
            else:
                DVE("tensor_copy", [psB[b]], [BhT], out=hT[:, kc, :], in_=pv[:, 0:512])

        if stage == 1:
            return finish()
        def fm_proj(g, b):
            for kc in range(8):
                PE("matmul", [BWb, BhT], [psB[b]], out=ps[b][:, :], lhsT=Wb[:, kc, g * 128:(g + 1) * 128], rhs=hT[:, kc, :], start=(kc == 0), stop=(kc == 7))

        for g in (0, 1):
            b = gbank()
            fm_proj(g, b)
            ACT("activation", [psB[b]], [Bsqb], out=sqb[:, :], in_=ps[b][:, :], func=AF.Square)
            b2 = gbank()
            PE("matmul", [Bsqb, Bcb], [psB[b2]], out=ps[b2][:, :], lhsT=blk_b, rhs=sqb[:, :], start=True, stop=True)
            rsqrt_act(rr[:, :], ps[b2][:, :], EPS_QK, [psB[b2]], [Brr])
            if g == 0:
                for m in range(2):
                    lo, hi = m * 64, (m + 1) * 64
                    DVE("scalar_tensor_tensor", [psB[b], Brr, Bpp], [BQT], out=QTp[m][lo:hi, :], in0=ps[b][lo:hi, :], scalar=pp[lo:hi, 16:17], in1=rr[lo:hi, :],
                        op0=ALU.mult, op1=ALU.mult)
            else:
                DVE("scalar_tensor_tensor", [psB[b], Brr, Bdv], [BKT[blk]], out=KT[:, blk * 512:(blk + 1) * 512], in0=ps[b][:, :], scalar=wk8, in1=rr[:, :],
                    op0=ALU.mult, op1=ALU.mult)

        for ci, g in enumerate((2, 3, 4)):
            b = gbank()
            fm_proj(g, b)
            cb_, Bc_ = cbuf[ci], Bcbf[ci]
            ACT("activation", [psB[b]], [Bc_], out=cb_[:, 3:515], in_=ps[b][:, :], func=AF.Copy)
            tap = 18 + 4 * ci
            tsmul(DVE, [Bc_, Bpp], [Bcy[ci]], cy[ci][:, :], cb_[:, 0:512], pp[:, tap:tap + 1])
            for j in range(1, 4):
                DVE("scalar_tensor_tensor", [Bc_, Bpp, Bcy[ci]], [Bcy[ci]], out=cy[ci][:, :], in0=cb_[:, j:j + 512], scalar=pp[:, tap + j:tap + j + 1],
                     in1=cy[ci][:, :], op0=ALU.mult, op1=ALU.add)
            POOL("tensor_copy", [Bc_, Bcy[ci]], [Bc_], out=cb_[:, 0:3], in_=cb_[:, 512:515])
            sigmoid_act(sg[ci][:, :], cy[ci][:, :], [Bcy[ci]], [Bsg[ci]])
            if ci == 2:
                DVE("tensor_tensor", [Bcy[ci], Bsg[ci]], [BvTd], out=vTd[:, :], in0=cy[ci][:, :], in1=sg[ci][:, :], op=ALU.mult)
            else:
                DVE("tensor_tensor", [Bcy[ci], Bsg[ci]], [Bcy[ci]], out=cy[ci][:, :], in0=cy[ci][:, :], in1=sg[ci][:, :], op=ALU.mult)
                ACT("activation", [Bcy[ci]], [Bsqb], out=sqb[:, :], in_=cy[ci][:, :], func=AF.Square)
                b2 = gbank()
                PE("matmul", [Bsqb, Bcb], [psB[b2]], out=ps[b2][:, :], lhsT=ones_b, rhs=sqb[:, :], start=True, stop=True)
                rsqrt_act(sg[ci][:, :], ps[b2][:, :], EPS_L2, [psB[b2]], [Bsg[ci]])
                col = 128 if ci == 0 else 0
                scl = float(128.0 ** -0.5) if ci == 0 else 1.0
                DVE("scalar_tensor_tensor", [Bcy[ci], Bsg[ci]], [Bkq], out=kq[:, :, col:col + 128], in0=cy[ci][:, :].rearrange("p (c t) -> p c t", c=4),
                    scalar=scl, in1=sg[ci][:, :].rearrange("p (c t) -> p c t", c=4), op0=ALU.mult, op1=ALU.mult)

        bab = gbank()
        for t in range(4):
            for kc in range(8):
                PE("matmul", [BWb, BhT], [psB[bab]], out=ps[bab][:, t * 2:t * 2 + 2], lhsT=hT[:, kc, t * 128:(t + 1) * 128], rhs=Wb[:, kc, 896:898],
                   start=(kc == 0), stop=(kc == 7), skip_group_check=True)
        ab3 = ps[bab][:, 0:8].rearrange("p (t c) -> p t c", c=2)
        ACT("activation", [psB[bab], Bpp], [Btk], out=tk[:, 32:36], in_=ab3[:, :, 0], func=AF.Exp, bias=dtb, scale=1.0)
        ACT("activation", [Btk, Beps], [Btk], out=tk[:, 32:36], in_=tk[:, 32:36], func=AF.Ln, bias=ONE, scale=1.0)
        tsmul(DVE, [Btk, Bdv], [Btk], tk[:, 0:4], tk[:, 32:36], negA)
        sigmoid_act(tk[:, 4:8], ab3[:, :, 1], [psB[bab]], [Btk])
        tsmul(DVE, [Btk], [Btk], tk[:, 28:32], tk[:, 4:8], -1.0)
        for half in range(2):
            b = gbank()
            for tt in range(2):
                t = half * 2 + tt
                for kc in range(8):
                    PE("matmul", [BWb, BhT], [psB[b]], out=ps[b][:, tt * 256:(tt + 1) * 256], lhsT=hT[:, kc, t * 128:(t + 1) * 128], rhs=Wb[:, kc, 640:896],
                       start=(kc == 0), stop=(kc == 7), skip_group_check=True)
            for tt in range(2):
                t = half * 2 + tt
                tile_i = blk * 4 + t
                ACT("activation", [psB[b]], [BV[blk]], out=Vaug[:, tile_i, 0:128], in_=ps[b][:, tt * 256:tt * 256 + 128], func=AF.Copy)
                zp = ps[b][:, tt * 256 + 128:tt * 256 + 256]
                sigmoid_act(zt[:, t, :], zp, [psB[b]], [Bzt])
                DVE("tensor_tensor", [psB[b], Bzt], [Bzt], out=zt[:, t, :], in0=zp, in1=zt[:, t, :], op=ALU.mult)
                POOL("tensor_tensor", [Bzt, Bdv], [Bgw], out=gw[:, t, :], in0=zt[:, t, :], in1=dn_wp, op=ALU.mult)

        if stage == 2:
            return finish()
        bsc = gbank()
        PE("matmul", [Btk, Bc], [psB[bsc]], out=ps[bsc][:, 0:4], lhsT=triU_f, rhs=tk[:, 0:4], start=True, stop=True, skip_group_check=True)
        PE("matmul", [Btk, Bc], [psB[bsc]], out=ps[bsc][:, 4:8], lhsT=ones_f, rhs=tk[:, 0:4], start=True, stop=True, skip_group_check=True)
        DVE("tensor_copy", [psB[bsc]], [Btk], out=tk[:, 8:12], in_=ps[bsc][:, 0:4])
        DVE("tensor_tensor", [psB[bsc], Btk], [Btk], out=tk[:, 12:16], in0=ps[bsc][:, 4:8], in1=tk[:, 8:12], op=ALU.subtract)
        ACT("activation", [Btk], [Btk], out=tk[:, 16:20], in_=tk[:, 8:12], func=AF.Exp)
        ACT("activation", [Btk], [Btk], out=tk[:, 20:24], in_=tk[:, 12:16], func=AF.Exp)
        ACT("activation", [psB[bsc]], [Btk], out=tk[:, 24:28], in_=ps[bsc][:, 4:8], func=AF.Exp)

        if stage == 21:
            return finish()
        def cv(c):
            return dict(kTc=kq[:, c, 0:128], qTc=kq[:, c, 128:256], gcol=tk[:, c:c + 1], beta=tk[:, 4 + c:5 + c], egc=tk[:, 16 + c:17 + c],
                        edec=tk[:, 20 + c:21 + c], eglast=tk[:, 24 + c:25 + c], negbeta=tk[:, 28 + c:29 + c])

        st = [dict() for _ in range(4)]
        for c in range(4):
            g, v = G[c], cv(c)
            b = gbank()
            pv = ps[b][:, :].bitcast(BF16)
            PE("transpose", [Bkq, Bcb], [psB[b]], out=pv[:, 0:128], in_=v["kTc"], identity=ident_b)
            PE("transpose", [BvTd, Bcb], [psB[b]], out=pv[:, 128:256], in_=vTd[:, c * 128:(c + 1) * 128], identity=ident_b)
            ACT("activation", [psB[b], Btk], [g["Bke"]], out=g["ke"][:, :], in_=pv[:, 0:128], func=AF.Copy, scale=v["egc"])
            tsmul(DVE, [psB[b], Btk], [g["Bkdec"]], g["kdec"][:, :], pv[:, 0:128], v["edec"])
            ACT("activation", [psB[b]], [g["Bvt"]], out=g["vt"][:, :], in_=pv[:, 128:256], func=AF.Copy)
            tsmul(DVE, [Bc, Btk], [g["BgTri"]], g["gTri"][:, :], triU_f, v["gcol"])
        for c in range(4):
            g, v = G[c], cv(c)
            bk = gbank()
            st[c]["bk"] = bk
            PE("matmul", [Bkq], [psB[bk]], out=ps[bk][:, 0:256], lhsT=v["kTc"], rhs=kq[:, c, :], start=True, stop=True)
            PE("matmul", [Bc, g["BgTri"]], [psB[bk]], out=ps[bk][:, 256:384], lhsT=ones_f, rhs=g["gTri"][:, :], start=False, stop=False, skip_group_check=True)
            PE("matmul", [Bc, g["BgTri"]], [psB[bk]], out=ps[bk][:, 256:384], lhsT=g["gTri"][:, :], rhs=negones_f, start=False, stop=False, skip_group_check=True)
            PE("matmul", [Bc], [psB[bk]], out=ps[bk][:, 256:384], lhsT=ident_f, rhs=NEG_f, start=False, stop=True, skip_group_check=True)
        for c in range(4):
            g, v = G[c], cv(c)
            bk = st[c]["bk"]
            ACT("activation", [psB[bk]], [g["BdecT"]], out=g["decT"][:, :], in_=ps[bk][:, 256:384], func=AF.Exp)
            DVE("tensor_tensor", [g["BdecT"], Bc], [g["BdecS"]], out=g["decS"][:, :], in0=g["decT"][:, :], in1=strictU_f, op=ALU.mult)
            DVE("scalar_tensor_tensor", [psB[bk], Btk, g["BdecS"]], [g["BNn"]], out=g["Nn"][:, :], in0=ps[bk][:, 0:128], scalar=v["negbeta"], in1=g["decS"][:, :],
                op0=ALU.mult, op1=ALU.mult)
            DVE("tensor_tensor", [psB[bk], g["BdecT"]], [g["Bqkm"]], out=g["qkm"][:, :], in0=ps[bk][:, 128:256], in1=g["decT"][:, :], op=ALU.mult)
        for c in range(4):
            g = G[c]
            bt = gbank()
            st[c]["bt"] = bt
            ptv = ps[bt][:, :].bitcast(BF16)
            PE("transpose", [g["BNn"], Bcb], [psB[bt]], out=ptv[:, 0:128], in_=g["Nn"][:, :], identity=ident_b)
        for c in range(4):
            g = G[c]
            ptv = ps[st[c]["bt"]][:, :].bitcast(BF16)
            if c % 2 == 0:
                DVE("tensor_copy", [psB[st[c]["bt"]]], [g["BNT"]], out=g["NT"][:, :], in_=ptv[:, 0:128])
            else:
                ACT("activation", [psB[st[c]["bt"]]], [g["BNT"]], out=g["NT"][:, :], in_=ptv[:, 0:128], func=AF.Copy)
            DVE("tensor_tensor", [g["BNn"], Bcb], [g["BX0"]], out=g["X0"][:, :], in0=g["Nn"][:, :], in1=ident_b, op=ALU.add)
            st[c].update(P=g["Nn"][:, :], PT=g["NT"][:, :], BP=[g["BNn"], g["BNT"]], X=g["X0"], BX=g["BX0"])
        if stage == 24:
            return finish()
        for lvl in range(6):
            for c in range(4):
                g, s_ = G[c], st[c]
                bp = gbank()
                s_["bp"] = bp
                PE("matmul", s_["BP"], [psB[bp]], out=ps[bp][:, 0:128], lhsT=s_["PT"], rhs=s_["P"], start=True, stop=True, skip_group_check=True)
                PE("matmul", s_["BP"], [psB[bp]], out=ps[bp][:, 128:256], lhsT=s_["P"], rhs=s_["PT"], start=True, stop=True, skip_group_check=True)
            for c in range(4):
                g, s_ = G[c], st[c]
                Pn, BPn = (g["P0"], g["BP0"]) if lvl % 2 == 0 else (g["P1"], g["BP1"])
                bp = s_["bp"]
                if c % 2 == 0:
                    ACT("activation", [psB[bp]], [BPn], out=Pn[:, :], in_=ps[bp][:, 0:256], func=AF.Copy)
                else:
                    DVE("tensor_copy", [psB[bp]], [BPn], out=Pn[:, :], in_=ps[bp][:, 0:256])
                s_["Pn"], s_["BPn"] = Pn, BPn
            for c in range(4):
                g, s_ = G[c], st[c]
                bx = gbank()
                s_["bx"] = bx
                PE("matmul", [Bcb, s_["BX"]], [psB[bx]], out=ps[bx][:, 0:128], lhsT=ident_b, rhs=s_["X"][:, :], start=True, stop=False)
                PE("matmul", [s_["BPn"], s_["BX"]], [psB[bx]], out=ps[bx][:, 0:128], lhsT=s_["Pn"][:, 128:256], rhs=s_["X"][:, :], start=False, stop=True)
            for c in range(4):
                g, s_ = G[c], st[c]
                Xn, BXn = (g["X1"], g["BX1"]) if lvl % 2 == 0 else (g["X0"], g["BX0"])
                bx = s_["bx"]
                if c % 2 == 0:
                    DVE("tensor_copy", [psB[bx]], [BXn], out=Xn[:, :], in_=ps[bx][:, 0:128])
                else:
                    ACT("activation", [psB[bx]], [BXn], out=Xn[:, :], in_=ps[bx][:, 0:128], func=AF.Copy)
                s_.update(P=s_["Pn"][:, 0:128], PT=s_["Pn"][:, 128:256], BP=[s_["BPn"]], X=Xn, BX=BXn)
        for c in range(4):
            g, v, s_ = G[c], cv(c), st[c]
            Xf, BXf = s_["X"], s_["BX"]
            bu = gbank()
            s_["bu"] = bu
            PE("matmul", [BXf, g["Bvt"]], [psB[bu]], out=ps[bu][:, 0:128], lhsT=Xf[:, :], rhs=g["vt"][:, :], start=True, stop=True, skip_group_check=True)
            PE("matmul", [BXf, g["Bke"]], [psB[bu]], out=ps[bu][:, 128:256], lhsT=g["ke"][:, :], rhs=Xf[:, :], start=True, stop=True, skip_group_check=True)
        for c in range(4):
            g, v, s_ = G[c], cv(c), st[c]
            bu = s_["bu"]
            ACT("activation", [psB[bu], Btk], [g["Bub"]], out=g["ub"][:, :], in_=ps[bu][:, 0:128], func=AF.Copy, scale=v["beta"])
            DVE("tensor_copy", [psB[bu]], [g["BwT"]], out=g["wT"][:, :], in_=ps[bu][:, 128:256])
        if stage == 25:
            return finish()
        for c in range(4):
            g, v = G[c], cv(c)
            ba = gbank()
            PE("matmul", [g["BwT"], BSbf], [psB[ba]], out=ps[ba][:, 0:128], lhsT=g["wT"][:, :], rhs=Sbf[:, :], start=True, stop=True, skip_group_check=True)
            PE("matmul", [Bkq, BSbf], [psB[ba]], out=ps[ba][:, 128:256], lhsT=v["qTc"], rhs=Sbf[:, :], start=True, stop=True, skip_group_check=True)
            DVE("scalar_tensor_tensor", [psB[ba], Btk, g["Bub"]], [g["Bvnew"]], out=g["vnew"][:, :], in0=ps[ba][:, 0:128], scalar=v["negbeta"], in1=g["ub"][:, :],
                op0=ALU.mult, op1=ALU.add)
            ACT("activation", [psB[ba], Btk], [g["BBs"]], out=g["Bs"][:, :], in_=ps[ba][:, 128:256], func=AF.Copy, scale=v["egc"])
            bs_ = gbank()
            PE("matmul", [g["Bkdec"], g["Bvnew"]], [psB[bs_]], out=ps[bs_][:, 0:128], lhsT=g["kdec"][:, :], rhs=g["vnew"][:, :], start=True, stop=True, skip_group_check=True)
            PE("matmul", [g["Bqkm"], g["Bvnew"]], [psB[bs_]], out=ps[bs_][:, 128:256], lhsT=g["qkm"][:, :], rhs=g["vnew"][:, :], start=True, stop=True, skip_group_check=True)
            DVE("scalar_tensor_tensor", [psB[bs_], Btk, BSst], [BSst], out=Sst[:, :], in0=Sst[:, :], scalar=v["eglast"], in1=ps[bs_][:, 0:128], op0=ALU.mult, op1=ALU.add)
            ACT("activation", [BSst], [BSbf], out=Sbf[:, :], in_=Sst[:, :], func=AF.Copy)
            DVE("tensor_tensor", [psB[bs_], g["BBs"]], [g["Bog"]], out=g["og"][:, :], in0=ps[bs_][:, 128:256], in1=g["Bs"][:, :], op=ALU.add)
            ACT("activation", [g["Bog"]], [g["BBs"], g["BstG"]], out=g["Bs"][:, :], in_=g["og"][:, :], func=AF.Square, accum_out=g["stG"][:, 0:1])
            rsqrt_act(g["stG"][:, 1:2], g["stG"][:, 0:1], EPS_O, [g["BstG"]], [g["BstG"]])
            DVE("scalar_tensor_tensor", [g["Bog"], g["BstG"], Bgw], [Bmixo[ms]], out=mixo[ms][:, c, 128:256], in0=g["og"][:, :], scalar=g["stG"][:, 1:2], in1=gw[:, c, :],
                op0=ALU.mult, op1=ALU.mult)

        if stage == 3:
            return finish()
        def Oreg(t, m):
            return ps[t][:, m * 129:(m + 1) * 129], psB[t]

        NEG_b = cstb[:, 512:640]
        for hh in range(2):
            H = blk * 2 + hh
            DVE("memset", [], [psB[0]], ap=ps[0][:, 0:258], constant=0.0)
            DVE("memset", [], [psB[1]], ap=ps[1][:, 0:258], constant=0.0)

            def qk(j):
                t_lo = 0 if j <= 2 * H else 1
                width = (2 - t_lo) * 128
                qc0 = hh * 256 + t_lo * 128
                sb = 2 + (stcnt[0] % 2)
                sl = stcnt[0] % 2
                stcnt[0] += 1
                diag = j >= 2 * H
                for m in range(2):
                    PE("matmul", [BKT[j // 4], BQT], [psB[sb]], out=ps[sb][:, m * 256 + t_lo * 128:m * 256 + 256], lhsT=KT[:, j * 128:(j + 1) * 128],
                       rhs=QTp[m][:, qc0:qc0 + width], start=True, stop=not diag, skip_group_check=True)
                    if diag:
                        td = j - 2 * H
                        PE("matmul", [Bcb], [psB[sb]], out=ps[sb][:, m * 256 + td * 128:m * 256 + td * 128 + 128], lhsT=ident_b, rhs=NEG_b,
                           start=False, stop=True, skip_group_check=True)
                return sb, sl, t_lo

            def rest(j, sb, sl, t_lo):
                if t_lo == 0:
                    ACT("activation", [psB[sb]], [BPT[sl]], out=PT[sl][:, :], in_=ps[sb][:, :], func=AF.Exp)
                else:
                    ACT("activation", [psB[sb]], [BPT[sl]], out=PT[sl][:, :].rearrange("p (m q) -> p m q", m=2)[:, :, 128:256],
                        in_=ps[sb][:, :].rearrange("p (m q) -> p m q", m=2)[:, :, 128:256], func=AF.Exp)
                for t in range(t_lo, 2):
                    for m in range(2):
                        oap, oB = Oreg(t, m)
                        PE("matmul", [BPT[sl], BV[j // 4]], [oB], out=oap, lhsT=PT[sl][:, m * 256 + t * 128:m * 256 + t * 128 + 128], rhs=Vaug[:, j, 0:129],
                           start=False, stop=False, skip_group_check=True)

            njs = 2 * H + 2
            cur = qk(0)
            for j in range(njs):
                nxt = qk(j + 1) if j + 1 < njs else None
                rest(j, *cur)
                cur = nxt
            for t in range(2):
                o0, B0 = Oreg(t, 0)
                o1, B1 = Oreg(t, 1)
                tl = hh * 2 + t
                DVE("reciprocal", [B0], [BstA], out=stA[:, 0:2], in_=ps[t][:, 128:258:129])
                DVE("tensor_tensor", [BstA, Bdv], [BstA], out=stA[:, 2:3], in0=stA[:, 1:2], in1=neglam, op=ALU.mult)
                ACT("activation", [B0, BstA], [Bot], out=ot[:, :], in_=o0[:, 0:128], func=AF.Copy, scale=stA[:, 0:1])
                DVE("scalar_tensor_tensor", [B1, BstA, Bot], [Boa], out=oa[:, :], in0=o1[:, 0:128], scalar=stA[:, 2:3], in1=ot[:, :], op0=ALU.mult, op1=ALU.add)
                ACT("activation", [Boa], [Bot, BstA], out=ot[:, :], in_=oa[:, :], func=AF.Square, accum_out=stA[:, 3:4])
                rsqrt_act(stA[:, 4:5], stA[:, 3:4], EPS_O, [BstA], [BstA])
                DVE("scalar_tensor_tensor", [Boa, BstA, Bdv], [Bmixo[ms]], out=mixo[ms][:, tl, 0:128], in0=oa[:, :], scalar=stA[:, 4:5], in1=da_wp, op0=ALU.mult, op1=ALU.mult)

        tok0 = blk * 512
        r_ = tok0 // TC
        i_ = (tok0 % TC) // PS
        row0 = r_ * PS + (tok0 % TC - i_ * PS)
        dst = mix_src[i_].ap()[row0:row0 + 512, :].rearrange("(t p) c -> p t c", p=128)
        DMA("pool", f"mixst{ms}", dst, mixo[ms][:, :, :], [Bmixo[ms]], [BMIX[i_]])

    if stage == 4:
        return finish()
    BALL = [Buf(f"mix_all{i}") for i in range(NCOL)]
    for i in range(NCOL):
        sc.op("pool", lambda e, i=i: e.collective_compute("AllGather", ALU.bypass, replica_groups=[[0, 1, 2, 3], [4, 5, 6, 7]],
                                                          ins=[mix_src[i].ap().opt()], outs=[mix_all[i].ap().opt()]),
              [BMIX[i]], [BALL[i]], dma=f"cc{i}", inc=1)
    if debug and NCOL == 1:
        DMA("pool", "dbg", dbg_d[:, :], mix_src[0].ap()[:, :], BMIX, [Buf("dbg")])

    if stage == 5:
        return finish()
    sc.barrier()
    A.off = mark_common
    glo[0] = 0
    WoB = A.alloc("WoB", [128, 8, D], BF16)
    WuB = A.alloc("WuB", [128, 8, 4 * D], BF16)
    WdB = A.alloc("WdB", [128, 32, D], BF16)
    BWo, BWu, BWd = Buf("WoB"), Buf("WuB"), Buf("WdB")
    wsg = [A.alloc(f"wsg{i}", [128, 1024], F32) for i in range(2)]
    Bwsg = [Buf(f"wsg{i}") for i in range(2)]
    xr = [A.alloc(f"xr{i}", [128, 2, D], F32) for i in range(2)]
    Bxr = [Buf(f"xr{i}") for i in range(2)]
    mtm = A.alloc("mtm", [128, 2, D], BF16)
    Bmtm = Buf("mtm")
    mT = A.alloc("mT", [128, 8, CB], BF16)
    BmT = Buf("mT")
    aT = A.alloc("aT", [128, 32, CB], BF16)
    BaT = Buf("aT")
    rl = [A.alloc(f"rl{i}", [128, CB], F32) for i in range(2)]
    Brl = [Buf(f"rl{i}") for i in range(2)]
    stC = A.alloc("stC", [128, 8], F32)
    BstC = Buf("stC")

    dyn = {}

    def load_off(e):
        reg = e.alloc_register("tokoff")
        ins = e.reg_load(reg, off_d[0:1, 0:1])
        dyn["v"] = e.snap(reg, min_val=0, max_val=3 * PS)
        return ins

    sc.op("pool", load_off, [], [])

    wcnt = [0]
    cast_engs = (DVE, POOL, ACT)

    def wload(dst_ap, src_ap, width, Bdst, scale_ap=None, shape3=None):
        i = wcnt[0] % 2
        eng = cast_engs[wcnt[0] % 3]
        wcnt[0] += 1
        stg = wsg[i][:, 0:width]
        if shape3 is not None:
            stg = stg.rearrange("p (f n) -> p f n", f=shape3)
        DMA("sp", f"wsg{i}", stg, src_ap, [], [Bwsg[i]])
        if scale_ap is not None:
            if eng is ACT:
                ACT("activation", [Bwsg[i], Bdv], [Bdst], out=dst_ap, in_=stg, func=AF.Copy, scale=scale_ap)
            else:
                tsmul(eng, [Bwsg[i], Bdv], [Bdst], dst_ap, stg, scale_ap)
        else:
            if eng is ACT:
                ACT("activation", [Bwsg[i]], [Bdst], out=dst_ap, in_=stg, func=AF.Copy)
            else:
                eng("tensor_copy", [Bwsg[i]], [Bdst], out=dst_ap, in_=stg)

    for kc in range(8):
        wload(WoB[:, kc, :], wout_d[kc * 128:(kc + 1) * 128, :], 1024, BWo)
    for kc in range(8):
        for qf in range(4):
            wload(WuB[:, kc, qf * 1024:(qf + 1) * 1024], wup_d[kc * 128:(kc + 1) * 128, qf * 1024:(qf + 1) * 1024], 1024, BWu, scale_ap=n2s[:, kc:kc + 1])
    for fc in range(32):
        wload(WdB[:, fc, :], wdn_d[fc * 128:(fc + 1) * 128, :], 1024, BWd)

    BMINE = Buf("mix_mine")
    for i in range(NCOL):
        for h4 in range(4):
            def mcopy(e, h4=h4, i=i):
                return e.dma_start(out=mix_mine.ap()[h4 * TC + i * PS:h4 * TC + (i + 1) * PS, :], in_=mix_all[i].ap()[bass.ds(dyn["v"] + h4 * 4 * PS, PS), :])
            sc.op("pool", mcopy, [BALL[i]], [BMINE], dma="mmine")
    mview = mix_mine.ap().rearrange("(h t) c -> t h c", h=4)
    for cb in range(NCB):
        xs = cb % 2
        xsrc = xres_d[cb * CB:(cb + 1) * CB, :].rearrange("(t p) d -> p t d", p=128)
        for tt in range(2):
            DMA("sp", f"xr{xs}", xr[xs][:, tt, :], xsrc[:, tt, :], [], [Bxr[xs]])
        for tt in range(2):
            DMA("sp", "mtm", mtm[:, tt, :].rearrange("p (h c) -> p h c", h=4), mview[cb * CB + tt * 128:cb * CB + (tt + 1) * 128, :, :], [BMINE], [Bmtm])
        for kc in range(8):
            b = gbank()
            pv = ps[b][:, :].bitcast(BF16)
            for tt in range(2):
                PE("transpose", [Bmtm, Bcb], [psB[b]], out=pv[:, tt * 128:(tt + 1) * 128], in_=mtm[:, tt, kc * 128:(kc + 1) * 128], identity=ident_b)
            if kc % 2 == 0:
                ACT("activation", [psB[b]], [BmT], out=mT[:, kc, :], in_=pv[:, 0:CB], func=AF.Copy)
            else:
                DVE("tensor_copy", [psB[b]], [BmT], out=mT[:, kc, :], in_=pv[:, 0:CB])
        for tt in range(2):
            for n in range(2):
                b = gbank()
                for kc in range(8):
                    PE("matmul", [BmT, BWo], [psB[b]], out=ps[b][:, :], lhsT=mT[:, kc, tt * 128:(tt + 1) * 128], rhs=WoB[:, kc, n * 512:(n + 1) * 512], start=(kc == 0), stop=(kc == 7))
                reg = xr[xs][:, tt, n * 512:(n + 1) * 512]
                DVE("tensor_tensor", [psB[b], Bxr[xs]], [Bxr[xs]], out=reg, in0=ps[b][:, :], in1=reg, op=ALU.add)
        for tt in range(2):
            ACT("activation", [Bxr[xs]], [Bmtm, BstC], out=mtm[:, tt, :], in_=xr[xs][:, tt, :], func=AF.Square, accum_out=stC[:, tt:tt + 1])
        rsqrt_act(stC[:, 2:4], stC[:, 0:2], EPS_X, [BstC], [BstC])
        for tt in range(2):
            tsmul(DVE, [Bxr[xs], BstC], [Bmtm], mtm[:, tt, :], xr[xs][:, tt, :], stC[:, 2 + tt:3 + tt])
        for kc in range(8):
            b = gbank()
            pv = ps[b][:, :].bitcast(BF16)
            for tt in range(2):
                PE("transpose", [Bmtm, Bcb], [psB[b]], out=pv[:, tt * 128:(tt + 1) * 128], in_=mtm[:, tt, kc * 128:(kc + 1) * 128], identity=ident_b)
            if kc % 2 == 0:
                ACT("activation", [psB[b]], [BmT], out=mT[:, kc, :], in_=pv[:, 0:CB], func=AF.Copy)
            else:
                DVE("tensor_copy", [psB[b]], [BmT], out=mT[:, kc, :], in_=pv[:, 0:CB])
        for fc in range(32):
            b = gbank()
            for kc in range(8):
                PE("matmul", [BmT, BWu], [psB[b]], out=ps[b][:, 0:CB], lhsT=WuB[:, kc, fc * 128:(fc + 1) * 128], rhs=mT[:, kc, :], start=(kc == 0), stop=(kc == 7))
            ri = fc % 2
            ACT("activation", [psB[b]], [Brl[ri]], out=rl[ri][:, :], in_=ps[b][:, 0:CB], func=AF.Relu)
            DVE("tensor_tensor", [Brl[ri]], [BaT], out=aT[:, fc, :], in0=rl[ri][:, :], in1=rl[ri][:, :], op=ALU.mult)
        for tt in range(2):
            for n in range(2):
                b = gbank()
                for fc in range(32):
                    PE("matmul", [BaT, BWd], [psB[b]], out=ps[b][:, :], lhsT=aT[:, fc, tt * 128:(tt + 1) * 128], rhs=WdB[:, fc, n * 512:(n + 1) * 512], start=(fc == 0), stop=(fc == 31))
                reg = xr[xs][:, tt, n * 512:(n + 1) * 512]
                DVE("tensor_tensor", [psB[b], Bxr[xs]], [Bxr[xs]], out=reg, in0=ps[b][:, :], in1=reg, op=ALU.add)
        ydst = y_d[cb * CB:(cb + 1) * CB, :].rearrange("(t p) d -> p t d", p=128)
        for tt in range(2):
            DMA("sp", f"yst{xs}", ydst[:, tt, :], xr[xs][:, tt, :], [Bxr[xs]], [Buf("y")])
    sc.barrier()

    return finish()


def _consts():
    j = np.arange(128)[:, None]
    i = np.arange(128)[None, :]
    ident = (i == j).astype(np.float32)
    triU = (j <= i).astype(np.float32)
    ones = np.ones((128, 128), np.float32)
    neg = np.where(i >= j, 0.0, NEGBIG).astype(np.float32)
    strictU = (i > j).astype(np.float32)
    blk = ((i // 64) == (j // 64)).astype(np.float32)
    return np.concatenate([ident, triU, ones, -ones, neg, strictU, blk], axis=1)


def make_in_maps(inp, S):
    x = np.asarray(inp["x"], np.float32)
    w_in = np.asarray(inp["w_in"], np.float32)[0]
    TC = S // 4
    cst = _consts()
    w_out = np.asarray(inp["w_out"], np.float32)[0]
    rows = np.concatenate([np.concatenate([np.arange(h * 128, (h + 1) * 128), 512 + np.arange(h * 128, (h + 1) * 128)]) for h in range(4)])
    w_out_p = np.ascontiguousarray(w_out[rows])
    w_up = np.ascontiguousarray(np.asarray(inp["w_up"], np.float32)[0])
    w_dn = np.ascontiguousarray(np.asarray(inp["w_down"], np.float32)[0])
    rep = lambda v: np.broadcast_to(np.asarray(v, np.float32).reshape(1, -1), (128, np.asarray(v).size))
    maps = []
    for core in range(8):
        b, h = core // 4, core % 4
        cols = np.concatenate([
            np.arange(h * 128, (h + 1) * 128),
            512 + np.arange(h * 128, (h + 1) * 128),
            1536 + np.arange(h * 128, (h + 1) * 128),
            2048 + np.arange(h * 128, (h + 1) * 128),
            2560 + np.arange(h * 128, (h + 1) * 128),
            1024 + np.arange(h * 128, (h + 1) * 128),
            3072 + np.arange(h * 128, (h + 1) * 128),
            np.array([3584 + h, 3588 + h]),
        ])
        w_c = np.ascontiguousarray(w_in[:, cols])
        cw = np.asarray(inp["conv_w"], np.float32)[0]
        taps = np.concatenate([cw[:, g * 512 + h * 128:g * 512 + (h + 1) * 128].T for g in range(3)], axis=1)
        pp = np.concatenate([
            np.asarray(inp["norm1_w"], np.float32)[0].reshape(8, 128).T,
            np.asarray(inp["norm2_w"], np.float32)[0].reshape(8, 128).T,
            np.tile(np.asarray(inp["q_norm_w"], np.float32)[0], 2).reshape(128, 1),
            np.tile(np.asarray(inp["k_norm_w"], np.float32)[0], 2).reshape(128, 1),
            taps,
            np.full((128, 1), np.asarray(inp["A_log"], np.float32)[0, h], np.float32),
            np.full((128, 1), np.asarray(inp["dt_bias"], np.float32)[0, h], np.float32),
            rep(inp["lambda_q1"][0]), rep(inp["lambda_k1"][0]), rep(inp["lambda_q2"][0]), rep(inp["lambda_k2"][0]),
            rep(inp["da_out_norm_w"][0]), rep(inp["dn_out_norm_w"][0]),
        ], axis=1).astype(np.float32)
        assert pp.shape == (128, 544), pp.shape
        maps.append({
            "x": np.ascontiguousarray(x[b, :S]),
            "xres": np.ascontiguousarray(x[b, h * TC:(h + 1) * TC]),
            "w_in": w_c,
            "consts": cst,
            "pp": np.ascontiguousarray(pp),
            "w_out": w_out_p,
            "w_up": w_up,
            "w_down": w_dn,
            "tokoff": np.array([[h * (TC // max(1, TC // 512))]], np.int32),
        })
    return maps


_CACHE = {}


def kernel(**inputs):
    S = 8192
    if "nc" not in _CACHE:
        _CACHE["nc"] = build_program(S)[0]
    nc = _CACHE["nc"]
    maps = make_in_maps(inputs, S)
    res = run_bass_kernel_spmd(nc, maps, core_ids=list(range(8)))
    TC = S // 4
    out = np.empty((2, S, D), np.float32)
    for core in range(8):
        b, r = core // 4, core % 4
        out[b, r * TC:(r + 1) * TC] = np.asarray(res.results[core]["y"], np.float32)
    return out
```

```python
import numpy as np
import ml_dtypes
import concourse.bass as bass
import concourse.mybir as mybir
from concourse.bass_utils import run_bass_kernel_spmd

F32 = mybir.dt.float32
BF16 = mybir.dt.bfloat16
I32 = mybir.dt.int32
AF = mybir.ActivationFunctionType
ALU = mybir.AluOpType
AX = mybir.AxisListType

D = 1024
EPS = 1e-6
LAM_INIT = 0.2
NEGBIG = -30000.0
ENG = ("pe", "act", "dve", "pool", "sp")
INTERLEAVE_GDN = False


class Buf:
    __slots__ = ("name", "w", "r", "excl")

    def __init__(self, name, excl=False):
        self.name = name
        self.w = None
        self.r = {}
        self.excl = excl


class Op:
    __slots__ = ("eng", "fn", "deps", "signal", "sigval", "dma", "chanval", "idx", "inc")


class Sched:
    def __init__(self):
        self.q = {e: [] for e in ENG}
        self.chans = {}
        self.n = 0

    def barrier(self):
        lasts = []
        for e in ENG:
            comp = [o for o in self.q[e] if o.dma is None and o.fn is not None]
            if comp:
                comp[-1].signal = True
                lasts.append(comp[-1])
        seen = {}
        for e in ENG:
            for o in self.q[e]:
                if o.dma is not None:
                    seen[o.dma] = o
        lasts += list(seen.values())
        for e in ENG:
            o = Op()
            o.eng = e
            o.fn = None
            o.signal = False
            o.sigval = 0
            o.dma = None
            o.chanval = 0
            o.inc = 0
            o.idx = self.n
            self.n += 1
            o.deps = [d for d in lasts if not (d.dma is None and d.eng == e)]
            self.q[e].append(o)

    def op(self, eng, fn, r=(), w=(), dma=None, inc=16):
        o = Op()
        o.inc = inc
        o.eng = eng
        o.fn = fn
        o.signal = False
        o.sigval = 0
        o.dma = dma
        o.chanval = 0
        o.idx = self.n
        self.n += 1
        if dma is not None:
            c = self.chans.setdefault(dma, [0])
            c[0] += 1
            c[0] += 0
            o.chanval = inc * c[0]
        deps = {}

        def add(d, raw):
            if d is None or d is o:
                return
            if d.dma is None and d.eng == eng and dma is None:
                if eng == "pe" or not raw:
                    return
            key = ("d", d.dma) if d.dma is not None else ("e", d.eng)
            cur = deps.get(key)
            if cur is None or cur.idx < d.idx:
                deps[key] = d

        for b in r:
            add(b.w, True)
            if b.excl:
                for x in b.r.values():
                    add(x, False)
        for b in w:
            add(b.w, False)
            for x in b.r.values():
                add(x, False)
        o.deps = list(deps.values())
        for d in o.deps:
            d.signal = True
        for b in r:
            key = ("d", dma, o.idx) if dma is not None else eng
            b.r[key] = o
        for b in w:
            b.w = o
            b.r = {}
        self.q[eng].append(o)
        return o

    def emit(self, nc, block, sems, chan_sems):
        for e in ENG:
            n = 0
            for o in self.q[e]:
                if o.dma is None and o.signal:
                    n += 1
                    o.sigval = n

        def run(engname):
            def body(eobj):
                waited = {}
                for o in self.q[engname]:
                    for d in o.deps:
                        if d.dma is not None:
                            key = ("d", d.dma)
                            sem = chan_sems[d.dma]
                            val = d.chanval
                        else:
                            key = ("e", d.eng)
                            sem = sems[d.eng]
                            val = d.sigval
                        if waited.get(key, 0) >= val:
                            continue
                        eobj.wait_ge(sem, val)
                        waited[key] = val
                    if o.fn is None:
                        continue
                    ins = o.fn(eobj)
                    if o.dma is not None:
                        ins.then_inc(chan_sems[o.dma], o.inc)
                    elif o.signal:
                        ins.then_inc(sems[engname], 1)
            return body

        block.tensor(run("pe"))
        block.scalar(run("act"))
        block.vector(run("dve"))
        block.gpsimd(run("pool"))
        block.sync(run("sp"))


class Arena:
    def __init__(self, nc, base=16640, limit=229376):
        self.nc = nc
        self.off = base
        self.limit = limit
        self.k = 0

    def alloc(self, name, shape, dtype):
        size = 2 if dtype == BF16 else 4
        n = 1
        for s in shape[1:]:
            n *= s
        nbytes = n * size
        self.off = (self.off + 63) // 64 * 64
        assert self.off + nbytes <= self.limit, f"SBUF overflow at {name}: {self.off + nbytes}"
        self.k += 1
        t = self.nc.alloc_sbuf_tensor_at(f"{name}_{self.k}", list(shape), dtype, offset=self.off)
        self.off += nbytes
        return t


def build_program(S=8192, debug=False, stage=99):
    nc = bass.Bass("TRN2", target_bir_lowering=False)
    NBLK = S // 512
    NT = S // 128
    TC = S // 4
    CB = 256
    NCB = TC // CB

    x_d = nc.dram_tensor("x", [S, D], F32, kind="ExternalInput").ap()
    xres_d = nc.dram_tensor("xres", [TC, D], F32, kind="ExternalInput").ap()
    win_d = nc.dram_tensor("w_in", [D, 898], F32, kind="ExternalInput").ap()
    cst_d = nc.dram_tensor("consts", [128, 896], F32, kind="ExternalInput").ap()
    pp_d = nc.dram_tensor("pp", [128, 544], F32, kind="ExternalInput").ap()
    wout_d = nc.dram_tensor("w_out", [D, D], F32, kind="ExternalInput").ap()
    wup_d = nc.dram_tensor("w_up", [D, 4 * D], F32, kind="ExternalInput").ap()
    wdn_d = nc.dram_tensor("w_down", [4 * D, D], F32, kind="ExternalInput").ap()
    off_d = nc.dram_tensor("tokoff", [1, 1], I32, kind="ExternalInput").ap()
    y_d = nc.dram_tensor("y", [TC, D], F32, kind="ExternalOutput").ap()
    NCOL = max(1, TC // 512)
    PS = TC // NCOL
    mix_src = [nc.dram_tensor(f"mix_src{i}", [4 * PS, 256], BF16) for i in range(NCOL)]
    mix_all = [nc.dram_tensor(f"mix_all{i}", [16 * PS, 256], BF16) for i in range(NCOL)]
    mix_mine = nc.dram_tensor("mix_mine", [4 * (S // 4), 256], BF16)
    if debug:
        dbg_d = nc.dram_tensor("dbg", [S, 256], BF16, kind="ExternalOutput").ap()

    sc = Sched()
    A = Arena(nc)


    def finish():
        sc.barrier()
        from contextlib import ExitStack
        with ExitStack() as es:
            sems = {e: es.enter_context(nc.semaphore(f"sem_{e}")) for e in ENG}
            chan_sems = {c: es.enter_context(nc.semaphore(f"ch_{c}")) for c in sc.chans}
            block = es.enter_context(nc.Block())
            sc.emit(nc, block, sems, chan_sems)
        info = dict(n_ops=sc.n, per_eng={e: len(sc.q[e]) for e in ENG}, off=A.off)
        return nc, info

    def mk(eng):
        def f(method, r, w, **kw):
            return sc.op(eng, lambda e, m=method, kw=kw: getattr(e, m)(**kw), r, w)
        return f

    PE, ACT, DVE, POOL = mk("pe"), mk("act"), mk("dve"), mk("pool")

    def DMA(q, chan, out, in_, r, w):
        return sc.op(q, lambda e, o=out, i=in_: e.dma_start(out=o, in_=i), r, w, dma=chan)

    ps = [nc.alloc_psum_tensor(f"ps{i}", [128, 512], F32) for i in range(8)]
    psB = [Buf(f"ps{i}", excl=True) for i in range(8)]
    gctr = [0]
    glo = [4]

    def gbank():
        n = 8 - glo[0]
        i = glo[0] + (gctr[0] % n)
        gctr[0] += 1
        return i

    cst = A.alloc("cst", [128, 896], F32)
    cstb = A.alloc("cstb", [128, 896], BF16)
    pp = A.alloc("pp", [128, 544], F32)
    dv = A.alloc("dv", [128, 320], F32)
    epsT = A.alloc("epsT", [128, 8], F32)
    tmp64 = A.alloc("tmp64", [128, 64], F32)
    Bc, Bcb, Bpp, Bdv, Beps, Btmp64 = (Buf(n) for n in ("cst", "cstb", "pp", "dv", "eps", "tmp64"))

    DMA("sp", "cst", cst[:, :], cst_d[:, :], [], [Bc])
    DMA("sp", "pp", pp[:, :], pp_d[:, :], [], [Bpp])
    DVE("tensor_copy", [Bc], [Bcb], out=cstb[:, :], in_=cst[:, :])
    ident_f, triU_f, ones_f, negones_f, NEG_f, strictU_f = (cst[:, i * 128:(i + 1) * 128] for i in range(6))
    ident_b = cstb[:, 0:128]
    triU_b = cstb[:, 128:256]
    ones_b = cstb[:, 256:384]
    blk_b = cstb[:, 768:896]

    epsvals = [1024 * EPS, 64 * EPS, EPS, 128 * EPS, 1.0]
    for i, v in enumerate(epsvals):
        DVE("memset", [], [Beps], ap=epsT[:, i:i + 1], constant=v)
    EPS_X, EPS_QK, EPS_L2, EPS_O, ONE = (epsT[:, i:i + 1] for i in range(5))

    def tsmul(eng, r, w, out, in0, s):
        return eng("tensor_scalar", r, w, out=out, in0=in0, scalar1=s, scalar2=None, op0=ALU.mult)

    tsmul(DVE, [Bpp], [Bdv], dv[:, 0:16], pp[:, 0:16], 32.0)
    tsmul(DVE, [Bpp], [Bdv], dv[:, 16:17], pp[:, 17:18], 8.0)
    ACT("activation", [Bpp], [Bdv], out=dv[:, 25:26], in_=pp[:, 30:31], func=AF.Exp)
    tsmul(DVE, [Bdv], [Bdv], dv[:, 17:18], dv[:, 25:26], -1.0)
    DVE("tensor_tensor", [Bpp], [Btmp64], out=tmp64[:, :], in0=pp[:, 32:96], in1=pp[:, 96:160], op=ALU.mult)
    DVE("reduce_sum", [Btmp64], [Bdv], out=dv[:, 20:21], in_=tmp64[:, :], axis=AX.X)
    DVE("tensor_tensor", [Bpp, Bdv], [Btmp64], out=tmp64[:, :], in0=pp[:, 160:224], in1=pp[:, 224:288], op=ALU.mult)
    DVE("reduce_sum", [Btmp64], [Bdv], out=dv[:, 21:22], in_=tmp64[:, :], axis=AX.X)
    ACT("activation", [Bdv], [Bdv], out=dv[:, 22:24], in_=dv[:, 20:22], func=AF.Exp)
    DVE("tensor_tensor", [Bdv], [Bdv], out=dv[:, 24:25], in0=dv[:, 23:24], in1=dv[:, 22:23], op=ALU.subtract)
    DVE("tensor_scalar", [Bdv], [Bdv], out=dv[:, 18:19], in0=dv[:, 24:25], scalar1=-LAM_INIT, scalar2=None, op0=ALU.add)
    tsmul(DVE, [Bpp], [Bdv], dv[:, 64:192], pp[:, 288:416], float(np.sqrt(128.0) * (1.0 - LAM_INIT)))
    tsmul(DVE, [Bpp], [Bdv], dv[:, 192:320], pp[:, 416:544], float(np.sqrt(128.0)))
    n1s = dv[:, 0:8]
    n2s = dv[:, 8:16]
    wq = pp[:, 16:17]
    wk8 = dv[:, 16:17]
    negA = dv[:, 17:18]
    neglam = dv[:, 18:19]
    dtb = pp[:, 31:32]
    da_wp = dv[:, 64:192]
    dn_wp = dv[:, 192:320]

    mark_common = A.off
    if stage == 0:
        return finish()

    def rsqrt_act(out_ap, in_ap, eps_ap, r, w):
        ACT("activation", r + [Beps], w, out=out_ap, in_=in_ap, func=AF.Ln, bias=eps_ap, scale=1.0)
        ACT("activation", w, w, out=out_ap, in_=out_ap, func=AF.Exp, scale=-0.5)

    def sigmoid_act(out_ap, in_ap, r, w):
        ACT("activation", r, w, out=out_ap, in_=in_ap, func=AF.Exp, scale=-1.0)
        ACT("activation", w + [Beps], w, out=out_ap, in_=out_ap, func=AF.Ln, bias=ONE, scale=1.0)
        ACT("activation", w, w, out=out_ap, in_=out_ap, func=AF.Exp, scale=-1.0)

    Wb = A.alloc("Wb", [128, 8, 898], BF16)
    wst = [A.alloc(f"wst{i}", [128, 898], F32) for i in range(2)]
    Bwst = [Buf(f"wst{i}") for i in range(2)]
    BWb = Buf("Wb")
    for kc in range(8):
        s = kc % 2
        DMA("sp", f"wst{s}", wst[s][:, :], win_d[kc * 128:(kc + 1) * 128, :], [], [Bwst[s]])
        tsmul(DVE, [Bwst[s], Bdv], [BWb], Wb[:, kc, :], wst[s][:, :], n1s[:, kc:kc + 1])

    xt = [A.alloc(f"xt{i}", [128, 4, D], F32) for i in range(2)]
    Bxt = [Buf(f"xt{i}") for i in range(2)]
    junk = A.alloc("junk", [128, D], BF16)
    Bjunk = Buf("junk")
    xb = A.alloc("xb", [128, 4, D], BF16)
    Bxb = Buf("xb")
    hT = A.alloc("hT", [128, 8, 512], BF16)
    BhT = Buf("hT")
    stx = A.alloc("stx", [128, 16], F32)
    Bstx = Buf("stx")
    KT = A.alloc("KT", [128, S], BF16)
    BKT = [Buf(f"KT{i}") for i in range(NBLK)]
    Vaug = A.alloc("Vaug", [128, NT, 130], BF16)
    BV = [Buf(f"V{i}") for i in range(NBLK)]
    QTp = [A.alloc(f"QTp{m}", [128, 512], BF16) for m in range(2)]
    BQT = Buf("QTp")
    sqb = A.alloc("sqb", [128, 512], BF16)
    Bsqb = Buf("sqb")
    rr = A.alloc("rr", [128, 512], F32)
    Brr = Buf("rr")
    cbuf = [A.alloc(f"cbuf{i}", [128, 515], F32) for i in range(3)]
    Bcbf = [Buf(f"cbuf{i}") for i in range(3)]
    cy = [A.alloc(f"cy{i}", [128, 512], F32) for i in range(3)]
    Bcy = [Buf(f"cy{i}") for i in range(3)]
    sg = [A.alloc(f"sg{i}", [128, 512], F32) for i in range(3)]
    Bsg = [Buf(f"sg{i}") for i in range(3)]
    kq = A.alloc("kq", [128, 4, 256], BF16)
    Bkq = Buf("kq")
    vTd = A.alloc("vTd", [128, 512], BF16)
    BvTd = Buf("vTd")
    gw = A.alloc("gw", [128, 4, 128], F32)
    Bgw = Buf("gw")
    zt = A.alloc("zt", [128, 4, 128], F32)
    Bzt = Buf("zt")
    tk = A.alloc("tk", [128, 64], F32)
    Btk = Buf("tk")
    PT = [A.alloc(f"PT{i}", [128, 512], BF16) for i in range(2)]
    BPT = [Buf(f"PT{i}") for i in range(2)]
    mixo = [A.alloc(f"mixo{i}", [128, 4, 256], BF16) for i in range(2)]
    Bmixo = [Buf(f"mixo{i}") for i in range(2)]
    ot = A.alloc("ot", [128, 128], F32)
    oa = A.alloc("oa", [128, 128], F32)
    Bot, Boa = Buf("ot"), Buf("oa")
    stA = A.alloc("stA", [128, 8], F32)
    BstA = Buf("stA")
    Sst = A.alloc("Sst", [128, 128], F32)
    Sbf = A.alloc("Sbf", [128, 128], BF16)
    BSst, BSbf = Buf("Sst"), Buf("Sbf")
    BMIX = [Buf(f"mix_src{i}") for i in range(NCOL)]
    G = []
    for p in range(4):
        g = {}
        for nm, dt_, w_ in (("ke", BF16, 128), ("kdec", BF16, 128), ("vt", BF16, 128), ("gTri", F32, 128),
                            ("decT", F32, 128), ("decS", F32, 128), ("Nn", BF16, 128), ("NT", BF16, 128),
                            ("P0", BF16, 256), ("P1", BF16, 256), ("X0", BF16, 128), ("X1", BF16, 128),
                            ("qkm", BF16, 128), ("ub", F32, 128), ("wT", BF16, 128), ("vnew", BF16, 128),
                            ("Bs", F32, 128), ("og", F32, 128), ("stG", F32, 8)):
            g[nm] = A.alloc(f"{nm}{p}", [128, w_], dt_)
            g["B" + nm] = Buf(f"{nm}{p}")
        G.append(g)
    phaseA_end = A.off

    for i in range(3):
        POOL("memset", [], [Bcbf[i]], ap=cbuf[i][:, 0:3], constant=0.0)
    POOL("memset", [], BV, ap=Vaug[:, :, 128:129], constant=1.0)
    DVE("memset", [], [BSst], ap=Sst[:, :], constant=0.0)
    DVE("memset", [], [BSbf], ap=Sbf[:, :], constant=0.0)
    for m in range(2):
        DVE("memset", [], [BQT], ap=QTp[m][:, :], constant=0.0)

    def load_x(blk):
        s = blk % 2
        src = x_d[blk * 512:(blk + 1) * 512, :].rearrange("(t p) d -> p t d", p=128)
        for t in range(4):
            DMA("sp", f"xt{s}", xt[s][:, t, :], src[:, t, :], [], [Bxt[s]])

    load_x(0)
    stcnt = [0]

    for blk in range(NBLK):
        s = blk % 2
        ms = blk % 2
        if blk + 1 < NBLK:
            load_x(blk + 1)
        for t in range(4):
            ACT("activation", [Bxt[s]], [Bjunk, Bstx], out=junk[:, :], in_=xt[s][:, t, :], func=AF.Square, accum_out=stx[:, t:t + 1])
        rsqrt_act(stx[:, 4:8], stx[:, 0:4], EPS_X, [Bstx], [Bstx])
        for t in range(4):
            tsmul(DVE, [Bxt[s], Bstx], [Bxb], xb[:, t, :], xt[s][:, t, :], stx[:, 4 + t:5 + t])
        for kc in range(8):
            b = gbank()
            pv = ps[b][:, :].bitcast(BF16)
            for t in range(4):
                PE("transpose", [Bxb, Bcb], [psB[b]], out=pv[:, t * 128:(t + 1) * 128], in_=xb[:, t, kc * 128:(kc + 1) * 128], identity=ident_b)
            if kc % 2 == 0:
                ACT("activation", [psB[b]], [BhT], out=hT[:, kc, :], in_=pv[:, 0:512], func=AF.Copy)
            else:
                DVE("tensor_copy", [psB[b]], [BhT], out=hT[:, kc, :], in_=pv[:, 0:512])

        if stage == 1:
            return finish()
        def fm_proj(g, b):
            for kc in range(8):
                PE("matmul", [BWb, BhT], [psB[b]], out=ps[b][:, :], lhsT=Wb[:, kc, g * 128:(g + 1) * 128], rhs=hT[:, kc, :], start=(kc == 0), stop=(kc == 7))

        for g in (0, 1):
            b = gbank()
            fm_proj(g, b)
            ACT("activation", [psB[b]], [Bsqb], out=sqb[:, :], in_=ps[b][:, :], func=AF.Square)
            b2 = gbank()
            PE("matmul", [Bsqb, Bcb], [psB[b2]], out=ps[b2][:, :], lhsT=blk_b, rhs=sqb[:, :], start=True, stop=True)
            rsqrt_act(rr[:, :], ps[b2][:, :], EPS_QK, [psB[b2]], [Brr])
            if g == 0:
                for m in range(2):
                    lo, hi = m * 64, (m + 1) * 64
                    DVE("scalar_tensor_tensor", [psB[b], Brr, Bpp], [BQT], out=QTp[m][lo:hi, :], in0=ps[b][lo:hi, :], scalar=pp[lo:hi, 16:17], in1=rr[lo:hi, :],
                        op0=ALU.mult, op1=ALU.mult)
            else:
                DVE("scalar_tensor_tensor", [psB[b], Brr, Bdv], [BKT[blk]], out=KT[:, blk * 512:(blk + 1) * 512], in0=ps[b][:, :], scalar=wk8, in1=rr[:, :],
                    op0=ALU.mult, op1=ALU.mult)

        for ci, g in enumerate((2, 3, 4)):
            b = gbank()
            fm_proj(g, b)
            cb_, Bc_ = cbuf[ci], Bcbf[ci]
            ACT("activation", [psB[b]], [Bc_], out=cb_[:, 3:515], in_=ps[b][:, :], func=AF.Copy)
            tap = 18 + 4 * ci
            tsmul(DVE, [Bc_, Bpp], [Bcy[ci]], cy[ci][:, :], cb_[:, 0:512], pp[:, tap:tap + 1])
            for j in range(1, 4):
                DVE("scalar_tensor_tensor", [Bc_, Bpp, Bcy[ci]], [Bcy[ci]], out=cy[ci][:, :], in0=cb_[:, j:j + 512], scalar=pp[:, tap + j:tap + j + 1],
                     in1=cy[ci][:, :], op0=ALU.mult, op1=ALU.add)
            POOL("tensor_copy", [Bc_, Bcy[ci]], [Bc_], out=cb_[:, 0:3], in_=cb_[:, 512:515])
            sigmoid_act(sg[ci][:, :], cy[ci][:, :], [Bcy[ci]], [Bsg[ci]])
            if ci == 2:
                DVE("tensor_tensor", [Bcy[ci], Bsg[ci]], [BvTd], out=vTd[:, :], in0=cy[ci][:, :], in1=sg[ci][:, :], op=ALU.mult)
            else:
                DVE("tensor_tensor", [Bcy[ci], Bsg[ci]], [Bcy[ci]], out=cy[ci][:, :], in0=cy[ci][:, :], in1=sg[ci][:, :], op=ALU.mult)
                ACT("activation", [Bcy[ci]], [Bsqb], out=sqb[:, :], in_=cy[ci][:, :], func=AF.Square)
                b2 = gbank()
                PE("matmul", [Bsqb, Bcb], [psB[b2]], out=ps[b2][:, :], lhsT=ones_b, rhs=sqb[:, :], start=True, stop=True)
                rsqrt_act(sg[ci][:, :], ps[b2][:, :], EPS_L2, [psB[b2]], [Bsg[ci]])
                col = 128 if ci == 0 else 0
                scl = float(128.0 ** -0.5) if ci == 0 else 1.0
                DVE("scalar_tensor_tensor", [Bcy[ci], Bsg[ci]], [Bkq], out=kq[:, :, col:col + 128], in0=cy[ci][:, :].rearrange("p (c t) -> p c t", c=4),
                    scalar=scl, in1=sg[ci][:, :].rearrange("p (c t) -> p c t", c=4), op0=ALU.mult, op1=ALU.mult)

        bab = gbank()
        for t in range(4):
            for kc in range(8):
                PE("matmul", [BWb, BhT], [psB[bab]], out=ps[bab][:, t * 2:t * 2 + 2], lhsT=hT[:, kc, t * 128:(t + 1) * 128], rhs=Wb[:, kc, 896:898],
                   start=(kc == 0), stop=(kc == 7), skip_group_check=True)
        ab3 = ps[bab][:, 0:8].rearrange("p (t c) -> p t c", c=2)
        ACT("activation", [psB[bab], Bpp], [Btk], out=tk[:, 32:36], in_=ab3[:, :, 0], func=AF.Exp, bias=dtb, scale=1.0)
        ACT("activation", [Btk, Beps], [Btk], out=tk[:, 32:36], in_=tk[:, 32:36], func=AF.Ln, bias=ONE, scale=1.0)
        tsmul(DVE, [Btk, Bdv], [Btk], tk[:, 0:4], tk[:, 32:36], negA)
        sigmoid_act(tk[:, 4:8], ab3[:, :, 1], [psB[bab]], [Btk])
        tsmul(DVE, [Btk], [Btk], tk[:, 28:32], tk[:, 4:8], -1.0)
        for half in range(2):
            b = gbank()
            for tt in range(2):
                t = half * 2 + tt
                for kc in range(8):
                    PE("matmul", [BWb, BhT], [psB[b]], out=ps[b][:, tt * 256:(tt + 1) * 256], lhsT=hT[:, kc, t * 128:(t + 1) * 128], rhs=Wb[:, kc, 640:896],
                       start=(kc == 0), stop=(kc == 7), skip_group_check=True)
            for tt in range(2):
                t = half * 2 + tt
                tile_i = blk * 4 + t
                ACT("activation", [psB[b]], [BV[blk]], out=Vaug[:, tile_i, 0:128], in_=ps[b][:, tt * 256:tt * 256 + 128], func=AF.Copy)
                zp = ps[b][:, tt * 256 + 128:tt * 256 + 256]
                sigmoid_act(zt[:, t, :], zp, [psB[b]], [Bzt])
                DVE("tensor_tensor", [psB[b], Bzt], [Bzt], out=zt[:, t, :], in0=zp, in1=zt[:, t, :], op=ALU.mult)
                POOL("tensor_tensor", [Bzt, Bdv], [Bgw], out=gw[:, t, :], in0=zt[:, t, :], in1=dn_wp, op=ALU.mult)

        if stage == 2:
            return finish()
        bsc = gbank()
        PE("matmul", [Btk, Bc], [psB[bsc]], out=ps[bsc][:, 0:4], lhsT=triU_f, rhs=tk[:, 0:4], start=True, stop=True, skip_group_check=True)
        PE("matmul", [Btk, Bc], [psB[bsc]], out=ps[bsc][:, 4:8], lhsT=ones_f, rhs=tk[:, 0:4], start=True, stop=True, skip_group_check=True)
        DVE("tensor_copy", [psB[bsc]], [Btk], out=tk[:, 8:12], in_=ps[bsc][:, 0:4])
        DVE("tensor_tensor", [psB[bsc], Btk], [Btk], out=tk[:, 12:16], in0=ps[bsc][:, 4:8], in1=tk[:, 8:12], op=ALU.subtract)
        ACT("activation", [Btk], [Btk], out=tk[:, 16:20], in_=tk[:, 8:12], func=AF.Exp)
        ACT("activation", [Btk], [Btk], out=tk[:, 20:24], in_=tk[:, 12:16], func=AF.Exp)
        ACT("activation", [psB[bsc]], [Btk], out=tk[:, 24:28], in_=ps[bsc][:, 4:8], func=AF.Exp)

        if stage == 21:
            return finish()
        def gdn_gen(blk=blk, ms=ms):
            def cv(c):
                return dict(kTc=kq[:, c, 0:128], qTc=kq[:, c, 128:256], gcol=tk[:, c:c + 1], beta=tk[:, 4 + c:5 + c], egc=tk[:, 16 + c:17 + c],
                            edec=tk[:, 20 + c:21 + c], eglast=tk[:, 24 + c:25 + c], negbeta=tk[:, 28 + c:29 + c])

            st = [dict() for _ in range(4)]
            for c in range(4):
                yield
                g, v = G[c], cv(c)
                b = gbank()
                pv = ps[b][:, :].bitcast(BF16)
                PE("transpose", [Bkq, Bcb], [psB[b]], out=pv[:, 0:128], in_=v["kTc"], identity=ident_b)
                PE("transpose", [BvTd, Bcb], [psB[b]], out=pv[:, 128:256], in_=vTd[:, c * 128:(c + 1) * 128], identity=ident_b)
                ACT("activation", [psB[b], Btk], [g["Bke"]], out=g["ke"][:, :], in_=pv[:, 0:128], func=AF.Copy, scale=v["egc"])
                tsmul(DVE, [psB[b], Btk], [g["Bkdec"]], g["kdec"][:, :], pv[:, 0:128], v["edec"])
                ACT("activation", [psB[b]], [g["Bvt"]], out=g["vt"][:, :], in_=pv[:, 128:256], func=AF.Copy)
                tsmul(DVE, [Bc, Btk], [g["BgTri"]], g["gTri"][:, :], triU_f, v["gcol"])
            for c in range(4):
                yield
                g, v = G[c], cv(c)
                bk = gbank()
                st[c]["bk"] = bk
                PE("matmul", [Bkq], [psB[bk]], out=ps[bk][:, 0:256], lhsT=v["kTc"], rhs=kq[:, c, :], start=True, stop=True)
                PE("matmul", [Bc, g["BgTri"]], [psB[bk]], out=ps[bk][:, 256:384], lhsT=ones_f, rhs=g["gTri"][:, :], start=False, stop=False, skip_group_check=True)
                PE("matmul", [Bc, g["BgTri"]], [psB[bk]], out=ps[bk][:, 256:384], lhsT=g["gTri"][:, :], rhs=negones_f, start=False, stop=False, skip_group_check=True)
                PE("matmul", [Bc], [psB[bk]], out=ps[bk][:, 256:384], lhsT=ident_f, rhs=NEG_f, start=False, stop=True, skip_group_check=True)
            for c in range(4):
                yield
                g, v = G[c], cv(c)
                bk = st[c]["bk"]
                ACT("activation", [psB[bk]], [g["BdecT"]], out=g["decT"][:, :], in_=ps[bk][:, 256:384], func=AF.Exp)
                DVE("tensor_tensor", [g["BdecT"], Bc], [g["BdecS"]], out=g["decS"][:, :], in0=g["decT"][:, :], in1=strictU_f, op=ALU.mult)
                DVE("scalar_tensor_tensor", [psB[bk], Btk, g["BdecS"]], [g["BNn"]], out=g["Nn"][:, :], in0=ps[bk][:, 0:128], scalar=v["negbeta"], in1=g["decS"][:, :],
                    op0=ALU.mult, op1=ALU.mult)
                DVE("tensor_tensor", [psB[bk], g["BdecT"]], [g["Bqkm"]], out=g["qkm"][:, :], in0=ps[bk][:, 128:256], in1=g["decT"][:, :], op=ALU.mult)
            for c in range(4):
                g = G[c]
                bt = gbank()
                st[c]["bt"] = bt
                ptv = ps[bt][:, :].bitcast(BF16)
                PE("transpose", [g["BNn"], Bcb], [psB[bt]], out=ptv[:, 0:128], in_=g["Nn"][:, :], identity=ident_b)
            for c in range(4):
                g = G[c]
                ptv = ps[st[c]["bt"]][:, :].bitcast(BF16)
                if c % 2 == 0:
                    DVE("tensor_copy", [psB[st[c]["bt"]]], [g["BNT"]], out=g["NT"][:, :], in_=ptv[:, 0:128])
                else:
                    ACT("activation", [psB[st[c]["bt"]]], [g["BNT"]], out=g["NT"][:, :], in_=ptv[:, 0:128], func=AF.Copy)
                DVE("tensor_tensor", [g["BNn"], Bcb], [g["BX0"]], out=g["X0"][:, :], in0=g["Nn"][:, :], in1=ident_b, op=ALU.add)
                st[c].update(P=g["Nn"][:, :], PT=g["NT"][:, :], BP=[g["BNn"], g["BNT"]], X=g["X0"], BX=g["BX0"])
            for lvl in range(6):
                yield
                for c in range(4):
                    g, s_ = G[c], st[c]
                    bp = gbank()
                    s_["bp"] = bp
                    PE("matmul", s_["BP"], [psB[bp]], out=ps[bp][:, 0:128], lhsT=s_["PT"], rhs=s_["P"], start=True, stop=True, skip_group_check=True)
                    PE("matmul", s_["BP"], [psB[bp]], out=ps[bp][:, 128:256], lhsT=s_["P"], rhs=s_["PT"], start=True, stop=True, skip_group_check=True)
                for c in range(4):
                    g, s_ = G[c], st[c]
                    Pn, BPn = (g["P0"], g["BP0"]) if lvl % 2 == 0 else (g["P1"], g["BP1"])
                    bp = s_["bp"]
                    if c % 2 == 0:
                        ACT("activation", [psB[bp]], [BPn], out=Pn[:, :], in_=ps[bp][:, 0:256], func=AF.Copy)
                    else:
                        DVE("tensor_copy", [psB[bp]], [BPn], out=Pn[:, :], in_=ps[bp][:, 0:256])
                    s_["Pn"], s_["BPn"] = Pn, BPn
                yield
                for c in range(4):
                    g, s_ = G[c], st[c]
                    bx = gbank()
                    s_["bx"] = bx
                    PE("matmul", [Bcb, s_["BX"]], [psB[bx]], out=ps[bx][:, 0:128], lhsT=ident_b, rhs=s_["X"][:, :], start=True, stop=False)
                    PE("matmul", [s_["BPn"], s_["BX"]], [psB[bx]], out=ps[bx][:, 0:128], lhsT=s_["Pn"][:, 128:256], rhs=s_["X"][:, :], start=False, stop=True)
                for c in range(4):
                    g, s_ = G[c], st[c]
                    Xn, BXn = (g["X1"], g["BX1"]) if lvl % 2 == 0 else (g["X0"], g["BX0"])
                    bx = s_["bx"]
                    if c % 2 == 0:
                        DVE("tensor_copy", [psB[bx]], [BXn], out=Xn[:, :], in_=ps[bx][:, 0:128])
                    else:
                        ACT("activation", [psB[bx]], [BXn], out=Xn[:, :], in_=ps[bx][:, 0:128], func=AF.Copy)
                    s_.update(P=s_["Pn"][:, 0:128], PT=s_["Pn"][:, 128:256], BP=[s_["BPn"]], X=Xn, BX=BXn)
            yield
            for c in range(4):
                g, v, s_ = G[c], cv(c), st[c]
                Xf, BXf = s_["X"], s_["BX"]
                bu = gbank()
                s_["bu"] = bu
                PE("matmul", [BXf, g["Bvt"]], [psB[bu]], out=ps[bu][:, 0:128], lhsT=Xf[:, :], rhs=g["vt"][:, :], start=True, stop=True, skip_group_check=True)
                PE("matmul", [BXf, g["Bke"]], [psB[bu]], out=ps[bu][:, 128:256], lhsT=g["ke"][:, :], rhs=Xf[:, :], start=True, stop=True, skip_group_check=True)
            for c in range(4):
                g, v, s_ = G[c], cv(c), st[c]
                bu = s_["bu"]
                ACT("activation", [psB[bu], Btk], [g["Bub"]], out=g["ub"][:, :], in_=ps[bu][:, 0:128], func=AF.Copy, scale=v["beta"])
                DVE("tensor_copy", [psB[bu]], [g["BwT"]], out=g["wT"][:, :], in_=ps[bu][:, 128:256])
            for c in range(4):
                yield
                g, v = G[c], cv(c)
                ba = gbank()
                PE("matmul", [g["BwT"], BSbf], [psB[ba]], out=ps[ba][:, 0:128], lhsT=g["wT"][:, :], rhs=Sbf[:, :], start=True, stop=True, skip_group_check=True)
                PE("matmul", [Bkq, BSbf], [psB[ba]], out=ps[ba][:, 128:256], lhsT=v["qTc"], rhs=Sbf[:, :], start=True, stop=True, skip_group_check=True)
                DVE("scalar_tensor_tensor", [psB[ba], Btk, g["Bub"]], [g["Bvnew"]], out=g["vnew"][:, :], in0=ps[ba][:, 0:128], scalar=v["negbeta"], in1=g["ub"][:, :],
                    op0=ALU.mult, op1=ALU.add)
                ACT("activation", [psB[ba], Btk], [g["BBs"]], out=g["Bs"][:, :], in_=ps[ba][:, 128:256], func=AF.Copy, scale=v["egc"])
                yield
                bs_ = gbank()
                PE("matmul", [g["Bkdec"], g["Bvnew"]], [psB[bs_]], out=ps[bs_][:, 0:128], lhsT=g["kdec"][:, :], rhs=g["vnew"][:, :], start=True, stop=True, skip_group_check=True)
                PE("matmul", [g["Bqkm"], g["Bvnew"]], [psB[bs_]], out=ps[bs_][:, 128:256], lhsT=g["qkm"][:, :], rhs=g["vnew"][:, :], start=True, stop=True, skip_group_check=True)
                DVE("scalar_tensor_tensor", [psB[bs_], Btk, BSst], [BSst], out=Sst[:, :], in0=Sst[:, :], scalar=v["eglast"], in1=ps[bs_][:, 0:128], op0=ALU.mult, op1=ALU.add)
                ACT("activation", [BSst], [BSbf], out=Sbf[:, :], in_=Sst[:, :], func=AF.Copy)
                DVE("tensor_tensor", [psB[bs_], g["BBs"]], [g["Bog"]], out=g["og"][:, :], in0=ps[bs_][:, 128:256], in1=g["Bs"][:, :], op=ALU.add)
                ACT("activation", [g["Bog"]], [g["BBs"], g["BstG"]], out=g["Bs"][:, :], in_=g["og"][:, :], func=AF.Square, accum_out=g["stG"][:, 0:1])
                rsqrt_act(g["stG"][:, 1:2], g["stG"][:, 0:1], EPS_O, [g["BstG"]], [g["BstG"]])
                DVE("scalar_tensor_tensor", [g["Bog"], g["BstG"], Bgw], [Bmixo[ms]], out=mixo[ms][:, c, 128:256], in0=g["og"][:, :], scalar=g["stG"][:, 1:2], in1=gw[:, c, :],
                    op0=ALU.mult, op1=ALU.mult)


        ggen = gdn_gen()
        if stage == 3 or not INTERLEAVE_GDN:
            for _ in ggen:
                pass
        if stage == 3:
            return finish()
        def Oreg(t, m):
            return ps[t][:, m * 129:(m + 1) * 129], psB[t]

        NEG_b = cstb[:, 512:640]
        for hh in range(2):
            H = blk * 2 + hh
            DVE("memset", [], [psB[0]], ap=ps[0][:, 0:258], constant=0.0)
            DVE("memset", [], [psB[1]], ap=ps[1][:, 0:258], constant=0.0)

            def qk(j):
                t_lo = 0 if j <= 2 * H else 1
                width = (2 - t_lo) * 128
                qc0 = hh * 256 + t_lo * 128
                sb = 2 + (stcnt[0] % 2)
                sl = stcnt[0] % 2
                stcnt[0] += 1
                diag = j >= 2 * H
                for m in range(2):
                    PE("matmul", [BKT[j // 4], BQT], [psB[sb]], out=ps[sb][:, m * 256 + t_lo * 128:m * 256 + 256], lhsT=KT[:, j * 128:(j + 1) * 128],
                       rhs=QTp[m][:, qc0:qc0 + width], start=True, stop=not diag, skip_group_check=True)
                    if diag:
                        td = j - 2 * H
                        PE("matmul", [Bcb], [psB[sb]], out=ps[sb][:, m * 256 + td * 128:m * 256 + td * 128 + 128], lhsT=ident_b, rhs=NEG_b,
                           start=False, stop=True, skip_group_check=True)
                return sb, sl, t_lo

            def rest(j, sb, sl, t_lo):
                if t_lo == 0:
                    ACT("activation", [psB[sb]], [BPT[sl]], out=PT[sl][:, :], in_=ps[sb][:, :], func=AF.Exp)
                else:
                    ACT("activation", [psB[sb]], [BPT[sl]], out=PT[sl][:, :].rearrange("p (m q) -> p m q", m=2)[:, :, 128:256],
                        in_=ps[sb][:, :].rearrange("p (m q) -> p m q", m=2)[:, :, 128:256], func=AF.Exp)
                for t in range(t_lo, 2):
                    for m in range(2):
                        oap, oB = Oreg(t, m)
                        PE("matmul", [BPT[sl], BV[j // 4]], [oB], out=oap, lhsT=PT[sl][:, m * 256 + t * 128:m * 256 + t * 128 + 128], rhs=Vaug[:, j, 0:129],
                           start=False, stop=False, skip_group_check=True)

            njs = 2 * H + 2
            cur = qk(0)
            for j in range(njs):
                nxt = qk(j + 1) if j + 1 < njs else None
                rest(j, *cur)
                cur = nxt
                next(ggen, None)
                if blk < 6:
                    next(ggen, None)
            for t in range(2):
                o0, B0 = Oreg(t, 0)
                o1, B1 = Oreg(t, 1)
                tl = hh * 2 + t
                DVE("reciprocal", [B0], [BstA], out=stA[:, 0:2], in_=ps[t][:, 128:258:129])
                DVE("tensor_tensor", [BstA, Bdv], [BstA], out=stA[:, 2:3], in0=stA[:, 1:2], in1=neglam, op=ALU.mult)
                ACT("activation", [B0, BstA], [Bot], out=ot[:, :], in_=o0[:, 0:128], func=AF.Copy, scale=stA[:, 0:1])
                DVE("scalar_tensor_tensor", [B1, BstA, Bot], [Boa], out=oa[:, :], in0=o1[:, 0:128], scalar=stA[:, 2:3], in1=ot[:, :], op0=ALU.mult, op1=ALU.add)
                ACT("activation", [Boa], [Bot, BstA], out=ot[:, :], in_=oa[:, :], func=AF.Square, accum_out=stA[:, 3:4])
                rsqrt_act(stA[:, 4:5], stA[:, 3:4], EPS_O, [BstA], [BstA])
                DVE("scalar_tensor_tensor", [Boa, BstA, Bdv], [Bmixo[ms]], out=mixo[ms][:, tl, 0:128], in0=oa[:, :], scalar=stA[:, 4:5], in1=da_wp, op0=ALU.mult, op1=ALU.mult)

        for _ in ggen:
            pass
        tok0 = blk * 512
        r_ = tok0 // TC
        i_ = (tok0 % TC) // PS
        row0 = r_ * PS + (tok0 % TC - i_ * PS)
        dst = mix_src[i_].ap()[row0:row0 + 512, :].rearrange("(t p) c -> p t c", p=128)
        DMA("pool", f"mixst{ms}", dst, mixo[ms][:, :, :], [Bmixo[ms]], [BMIX[i_]])

    if stage == 4:
        return finish()
    sc.barrier()
    BALL = [Buf(f"mix_all{i}") for i in range(NCOL)]
    for i in range(NCOL):
        sc.op("pool", lambda e, i=i: e.collective_compute("AllGather", ALU.bypass, replica_groups=[[0, 1, 2, 3], [4, 5, 6, 7]],
                                                          ins=[mix_src[i].ap().opt()], outs=[mix_all[i].ap().opt()]),
              [BMIX[i]], [BALL[i]], dma=f"cc{i}", inc=1)
    if debug and NCOL == 1:
        DMA("pool", "dbg", dbg_d[:, :], mix_src[0].ap()[:, :], BMIX, [Buf("dbg")])

    if stage == 5:
        return finish()
    A.off = mark_common
    glo[0] = 0
    WoB = A.alloc("WoB", [128, 8, D], BF16)
    WuB = A.alloc("WuB", [128, 8, 4 * D], BF16)
    WdB = A.alloc("WdB", [128, 32, D], BF16)
    BWo, BWu, BWd = Buf("WoB"), Buf("WuB"), Buf("WdB")
    wsg = [A.alloc(f"wsg{i}", [128, 1024], F32) for i in range(2)]
    Bwsg = [Buf(f"wsg{i}") for i in range(2)]
    xr = [A.alloc(f"xr{i}", [128, 2, D], F32) for i in range(2)]
    Bxr = [Buf(f"xr{i}") for i in range(2)]
    mtm = A.alloc("mtm", [128, 2, D], BF16)
    Bmtm = Buf("mtm")
    mT = A.alloc("mT", [128, 8, CB], BF16)
    BmT = Buf("mT")
    aT = A.alloc("aT", [128, 32, CB], BF16)
    BaT = Buf("aT")
    rl = [A.alloc(f"rl{i}", [128, CB], F32) for i in range(2)]
    Brl = [Buf(f"rl{i}") for i in range(2)]
    stC = A.alloc("stC", [128, 8], F32)
    BstC = Buf("stC")

    dyn = {}

    def load_off(e):
        reg = e.alloc_register("tokoff")
        ins = e.reg_load(reg, off_d[0:1, 0:1])
        dyn["v"] = e.snap(reg, min_val=0, max_val=3 * PS)
        return ins

    sc.op("pool", load_off, [], [])

    wcnt = [0]
    cast_engs = (DVE, ACT)

    def wload(dst_ap, src_ap, width, Bdst, scale_ap=None, shape3=None):
        i = (wcnt[0] // 2) % 2 if False else wcnt[0] % 2
        eng = cast_engs[wcnt[0] % 2]
        wcnt[0] += 1
        stg = wsg[i][:, 0:width]
        if shape3 is not None:
            stg = stg.rearrange("p (f n) -> p f n", f=shape3)
        DMA("sp", f"wsg{i}", stg, src_ap, [], [Bwsg[i]])
        if scale_ap is not None:
            if eng is ACT:
                ACT("activation", [Bwsg[i], Bdv], [Bdst], out=dst_ap, in_=stg, func=AF.Copy, scale=scale_ap)
            else:
                tsmul(eng, [Bwsg[i], Bdv], [Bdst], dst_ap, stg, scale_ap)
        else:
            if eng is ACT:
                ACT("activation", [Bwsg[i]], [Bdst], out=dst_ap, in_=stg, func=AF.Copy)
            else:
                eng("tensor_copy", [Bwsg[i]], [Bdst], out=dst_ap, in_=stg)

    for kc in range(8):
        wload(WoB[:, kc, :], wout_d[kc * 128:(kc + 1) * 128, :], 1024, BWo)
    for kc in range(8):
        for qf in range(4):
            wload(WuB[:, kc, qf * 1024:(qf + 1) * 1024], wup_d[kc * 128:(kc + 1) * 128, qf * 1024:(qf + 1) * 1024], 1024, BWu, scale_ap=n2s[:, kc:kc + 1])
    for fc in range(32):
        wload(WdB[:, fc, :], wdn_d[fc * 128:(fc + 1) * 128, :], 1024, BWd)

    BMINE = Buf("mix_mine")
    for i in range(NCOL):
        for h4 in range(4):
            def mcopy(e, h4=h4, i=i):
                return e.dma_start(out=mix_mine.ap()[h4 * TC + i * PS:h4 * TC + (i + 1) * PS, :], in_=mix_all[i].ap()[bass.ds(dyn["v"] + h4 * 4 * PS, PS), :])
            sc.op("pool", mcopy, [BALL[i]], [BMINE], dma="mmine")
    mview = mix_mine.ap().rearrange("(h t) c -> t h c", h=4)
    for cb in range(NCB):
        xs = cb % 2
        xsrc = xres_d[cb * CB:(cb + 1) * CB, :].rearrange("(t p) d -> p t d", p=128)
        for tt in range(2):
            DMA("sp", f"xr{xs}", xr[xs][:, tt, :], xsrc[:, tt, :], [], [Bxr[xs]])
        for tt in range(2):
            DMA("sp", "mtm", mtm[:, tt, :].rearrange("p (h c) -> p h c", h=4), mview[cb * CB + tt * 128:cb * CB + (tt + 1) * 128, :, :], [BMINE], [Bmtm])
        for kc in range(8):
            b = gbank()
            pv = ps[b][:, :].bitcast(BF16)
            for tt in range(2):
                PE("transpose", [Bmtm, Bcb], [psB[b]], out=pv[:, tt * 128:(tt + 1) * 128], in_=mtm[:, tt, kc * 128:(kc + 1) * 128], identity=ident_b)
            if kc % 2 == 0:
                ACT("activation", [psB[b]], [BmT], out=mT[:, kc, :], in_=pv[:, 0:CB], func=AF.Copy)
            else:
                DVE("tensor_copy", [psB[b]], [BmT], out=mT[:, kc, :], in_=pv[:, 0:CB])
        for tt in range(2):
            for n in range(2):
                b = gbank()
                for kc in range(8):
                    PE("matmul", [BmT, BWo], [psB[b]], out=ps[b][:, :], lhsT=mT[:, kc, tt * 128:(tt + 1) * 128], rhs=WoB[:, kc, n * 512:(n + 1) * 512], start=(kc == 0), stop=(kc == 7))
                reg = xr[xs][:, tt, n * 512:(n + 1) * 512]
                DVE("tensor_tensor", [psB[b], Bxr[xs]], [Bxr[xs]], out=reg, in0=ps[b][:, :], in1=reg, op=ALU.add)
        for tt in range(2):
            ACT("activation", [Bxr[xs]], [Bmtm, BstC], out=mtm[:, tt, :], in_=xr[xs][:, tt, :], func=AF.Square, accum_out=stC[:, tt:tt + 1])
        rsqrt_act(stC[:, 2:4], stC[:, 0:2], EPS_X, [BstC], [BstC])
        for tt in range(2):
            tsmul(DVE, [Bxr[xs], BstC], [Bmtm], mtm[:, tt, :], xr[xs][:, tt, :], stC[:, 2 + tt:3 + tt])
        for kc in range(8):
            b = gbank()
            pv = ps[b][:, :].bitcast(BF16)
            for tt in range(2):
                PE("transpose", [Bmtm, Bcb], [psB[b]], out=pv[:, tt * 128:(tt + 1) * 128], in_=mtm[:, tt, kc * 128:(kc + 1) * 128], identity=ident_b)
            if kc % 2 == 0:
                ACT("activation", [psB[b]], [BmT], out=mT[:, kc, :], in_=pv[:, 0:CB], func=AF.Copy)
            else:
                DVE("tensor_copy", [psB[b]], [BmT], out=mT[:, kc, :], in_=pv[:, 0:CB])
        for fc in range(32):
            b = gbank()
            for kc in range(8):
                PE("matmul", [BmT, BWu], [psB[b]], out=ps[b][:, 0:CB], lhsT=WuB[:, kc, fc * 128:(fc + 1) * 128], rhs=mT[:, kc, :], start=(kc == 0), stop=(kc == 7))
            ri = fc % 2
            ACT("activation", [psB[b]], [Brl[ri]], out=rl[ri][:, :], in_=ps[b][:, 0:CB], func=AF.Relu)
            DVE("tensor_tensor", [Brl[ri]], [BaT], out=aT[:, fc, :], in0=rl[ri][:, :], in1=rl[ri][:, :], op=ALU.mult)
        for tt in range(2):
            for n in range(2):
                b = gbank()
                for fc in range(32):
                    PE("matmul", [BaT, BWd], [psB[b]], out=ps[b][:, :], lhsT=aT[:, fc, tt * 128:(tt + 1) * 128], rhs=WdB[:, fc, n * 512:(n + 1) * 512], start=(fc == 0), stop=(fc == 31))
                reg = xr[xs][:, tt, n * 512:(n + 1) * 512]
                DVE("tensor_tensor", [psB[b], Bxr[xs]], [Bxr[xs]], out=reg, in0=ps[b][:, :], in1=reg, op=ALU.add)
        ydst = y_d[cb * CB:(cb + 1) * CB, :].rearrange("(t p) d -> p t d", p=128)
        for tt in range(2):
            DMA("sp", f"yst{xs}", ydst[:, tt, :], xr[xs][:, tt, :], [Bxr[xs]], [Buf("y")])
    sc.barrier()

    return finish()


def _consts():
    j = np.arange(128)[:, None]
    i = np.arange(128)[None, :]
    ident = (i == j).astype(np.float32)
    triU = (j <= i).astype(np.float32)
    ones = np.ones((128, 128), np.float32)
    neg = np.where(i >= j, 0.0, NEGBIG).astype(np.float32)
    strictU = (i > j).astype(np.float32)
    blk = ((i // 64) == (j // 64)).astype(np.float32)
    return np.concatenate([ident, triU, ones, -ones, neg, strictU, blk], axis=1)


def make_in_maps(inp, S):
    x = np.asarray(inp["x"], np.float32)
    w_in = np.asarray(inp["w_in"], np.float32)[0]
    TC = S // 4
    cst = _consts()
    w_out = np.asarray(inp["w_out"], np.float32)[0]
    rows = np.concatenate([np.concatenate([np.arange(h * 128, (h + 1) * 128), 512 + np.arange(h * 128, (h + 1) * 128)]) for h in range(4)])
    w_out_p = np.ascontiguousarray(w_out[rows])
    w_up = np.ascontiguousarray(np.asarray(inp["w_up"], np.float32)[0])
    w_dn = np.ascontiguousarray(np.asarray(inp["w_down"], np.float32)[0])
    rep = lambda v: np.broadcast_to(np.asarray(v, np.float32).reshape(1, -1), (128, np.asarray(v).size))
    maps = []
    for core in range(8):
        b, h = core // 4, core % 4
        cols = np.concatenate([
            np.arange(h * 128, (h + 1) * 128),
            512 + np.arange(h * 128, (h + 1) * 128),
            1536 + np.arange(h * 128, (h + 1) * 128),
            2048 + np.arange(h * 128, (h + 1) * 128),
            2560 + np.arange(h * 128, (h + 1) * 128),
            1024 + np.arange(h * 128, (h + 1) * 128),
            3072 + np.arange(h * 128, (h + 1) * 128),
            np.array([3584 + h, 3588 + h]),
        ])
        w_c = np.ascontiguousarray(w_in[:, cols])
        cw = np.asarray(inp["conv_w"], np.float32)[0]
        taps = np.concatenate([cw[:, g * 512 + h * 128:g * 512 + (h + 1) * 128].T for g in range(3)], axis=1)
        pp = np.concatenate([
            np.asarray(inp["norm1_w"], np.float32)[0].reshape(8, 128).T,
            np.asarray(inp["norm2_w"], np.float32)[0].reshape(8, 128).T,
            np.tile(np.asarray(inp["q_norm_w"], np.float32)[0], 2).reshape(128, 1),
            np.tile(np.asarray(inp["k_norm_w"], np.float32)[0], 2).reshape(128, 1),
            taps,
            np.full((128, 1), np.asarray(inp["A_log"], np.float32)[0, h], np.float32),
            np.full((128, 1), np.asarray(inp["dt_bias"], np.float32)[0, h], np.float32),
            rep(inp["lambda_q1"][0]), rep(inp["lambda_k1"][0]), rep(inp["lambda_q2"][0]), rep(inp["lambda_k2"][0]),
            rep(inp["da_out_norm_w"][0]), rep(inp["dn_out_norm_w"][0]),
        ], axis=1).astype(np.float32)
        assert pp.shape == (128, 544), pp.shape
        maps.append({
            "x": np.ascontiguousarray(x[b, :S]),
            "xres": np.ascontiguousarray(x[b, h * TC:(h + 1) * TC]),
            "w_in": w_c,
            "consts": cst,
            "pp": np.ascontiguousarray(pp),
            "w_out": w_out_p,
            "w_up": w_up,
            "w_down": w_dn,
            "tokoff": np.array([[h * (TC // max(1, TC // 512))]], np.int32),
        })
    return maps


_CACHE = {}


def kernel(**inputs):
    S = 8192
    if "nc" not in _CACHE:
        _CACHE["nc"] = build_program(S)[0]
    nc = _CACHE["nc"]
    maps = make_in_maps(inputs, S)
    res = run_bass_kernel_spmd(nc, maps, core_ids=list(range(8)))
    TC = S // 4
    out = np.empty((2, S, D), np.float32)
    for core in range(8):
        b, r = core // 4, core % 4
        out[b, r * TC:(r + 1) * TC] = np.asarray(res.results[core]["y"], np.float32)
    return out
```
